# Optimizing a Trainium2 kernel written in Bass

```python
import jax, jax.numpy as jnp
from jax import lax
import numpy as np

D_MODEL = 2048
BATCH = 16
SEQ = 256
DEPTH = 1
DEC_BATCH = 8
DEC_SEQ = 4096
PAST_LEN = 256

GRID_W = 64
HEAD_SIZE = 64
D_RWKV = D_MODEL // 2
N_HEADS_RWKV = D_RWKV // HEAD_SIZE
DECAY_LORA = 64
AAA_LORA = 64
GATE_LORA = 128
D_POOL = D_MODEL - D_RWKV
POOL_WINDOWS = (2, 4, 8, 16)
N_POOL_GROUPS = len(POOL_WINDOWS)
POOL_GROUP = D_POOL // N_POOL_GROUPS
D_FF = -(-8 * D_MODEL // (3 * 256)) * 256
RWKV_COLS = 3 * D_RWKV + 2 * DECAY_LORA + 2 * AAA_LORA + GATE_LORA
IN_COLS = RWKV_COLS + D_POOL + 2 * D_MODEL
RWKV_SPLITS = [D_RWKV, 2 * D_RWKV, 3 * D_RWKV, 3 * D_RWKV + DECAY_LORA, 3 * D_RWKV + 2 * DECAY_LORA,
               3 * D_RWKV + 2 * DECAY_LORA + AAA_LORA, 3 * D_RWKV + 2 * DECAY_LORA + 2 * AAA_LORA]
NORM_EPS = 1e-6
GN_EPS = 64e-5

kernel_name = 'bidir_rwkv7_pool_prefix_dit_step'

F32 = jnp.float32


def rms_norm(x, g):
    xf = x.astype(F32)
    y = xf * lax.rsqrt(jnp.mean(xf * xf, axis=-1, keepdims=True) + NORM_EPS)
    return (y * g.astype(F32)).astype(x.dtype)


def centred_shift(u):
    zero = jnp.zeros_like(u[:, :1])
    prev = jnp.concatenate([zero, u[:, :-1]], axis=1)
    nxt = jnp.concatenate([u[:, 1:], zero], axis=1)
    return 0.5 * (prev + nxt) - u


def window_mean(u, w, axis):
    n = u.shape[axis]
    cs = jnp.cumsum(u.astype(F32), axis=axis)
    pad = [(0, 0)] * u.ndim
    pad[axis] = (1, 0)
    cs = jnp.pad(cs, pad)
    t = jnp.arange(n)
    lo = jnp.clip(t - w // 2, 0, n)
    hi = jnp.clip(t - w // 2 + w, 0, n)
    s = jnp.take(cs, hi, axis=axis) - jnp.take(cs, lo, axis=axis)
    shape = [1] * u.ndim
    shape[axis] = n
    return s / (hi - lo).astype(F32).reshape(shape)


def wkv_step(S, inp):
    r, w, k, v, kk, a = inp
    sa = jnp.einsum('bhvk,bhk->bhv', S, -kk)
    S = S * w[:, :, None, :] + sa[..., None] * (kk * a)[:, :, None, :] + v[..., None] * k[:, :, None, :]
    return S, jnp.einsum('bhvk,bhk->bhv', S, r)


def wkv_scan(S0, r, w, k, v, kk, a, reverse):
    xs = tuple(jnp.swapaxes(u, 0, 1) for u in (r, w, k, v, kk, a))
    S, ys = lax.scan(wkv_step, S0, xs, reverse=reverse)
    return S, jnp.swapaxes(ys, 0, 1)


def rwkv7_bidir(z, S0, lp):
    B, T, _ = z.shape
    H, N = N_HEADS_RWKV, HEAD_SIZE
    zf = z.astype(F32)
    r, k, v, wd_f, wd_b, ad_f, ad_b, gd = jnp.split(zf, RWKV_SPLITS, axis=-1)

    def heads(u):
        return u.reshape(B, T, H, N)

    kk = heads(k * lp['k_k'].astype(F32))
    kk = kk / jnp.maximum(jnp.sqrt(jnp.sum(kk * kk, axis=-1, keepdims=True)), 1e-12)
    r_h = heads(r)
    v_h = heads(v)
    g = jax.nn.sigmoid(gd) @ lp['g2'].astype(F32)
    k_a = lp['k_a'].astype(F32)
    r_k = lp['r_k'].astype(F32)
    outs, bonuses, finals = [], [], []
    for d, (wd, ad, rev) in enumerate(((wd_f, ad_f, False), (wd_b, ad_b, True))):
        w_log = -jax.nn.softplus(-(lp['w0'][d].astype(F32) + jnp.tanh(wd) @ lp['w2'][d].astype(F32))) - 0.5
        decay = jnp.exp(-jnp.exp(w_log))
        a = jax.nn.sigmoid(lp['a0'][d].astype(F32) + ad @ lp['a2'][d].astype(F32))
        k_d = heads(k * (1.0 + (a - 1.0) * k_a))
        S_d, y_d = wkv_scan(S0[:, d].astype(F32), r_h, heads(decay), k_d, v_h, kk, heads(a), rev)
        outs.append(y_d)
        bonuses.append(jnp.sum(r_h * k_d * r_k, axis=-1, keepdims=True) * v_h)
        finals.append(S_d)
    y = outs[0] + outs[1]
    mu = jnp.mean(y, axis=-1, keepdims=True)
    var = jnp.mean(jnp.square(y - mu), axis=-1, keepdims=True)
    y = (y - mu) * lax.rsqrt(var + GN_EPS) * lp['lnx_g'].astype(F32).reshape(H, N) + lp['lnx_b'].astype(F32).reshape(H, N)
    y = (y + bonuses[0] + bonuses[1]).reshape(B, T, D_RWKV) * g
    return y.astype(z.dtype), jnp.stack(finals, axis=1)


def pool_mixer(z, rows, lp):
    B, T, _ = z.shape
    outs = []
    for gi, w in enumerate(POOL_WINDOWS):
        zg = z[..., gi * POOL_GROUP:(gi + 1) * POOL_GROUP]
        if rows is None:
            m = window_mean(zg, w, 1)
        else:
            grid = zg.reshape(B, rows, GRID_W, POOL_GROUP)
            m = window_mean(window_mean(grid, w, 2), w, 1).reshape(B, T, POOL_GROUP)
        d = (m - zg.astype(F32)).astype(z.dtype)
        outs.append(d @ lp['pool_w'][gi])
    return jnp.concatenate(outs, axis=-1) * lp['pool_scale']


def trunk_layer(x, mod, S0, rows, lp):
    sh1, sc1, g1, sh2, sc2, g2 = jnp.split(mod[:, None, :], 6, axis=-1)
    h = rms_norm(x, lp['norm1_g']) * (1.0 + sc1) + sh1
    proj = h @ lp['w_in']
    za = proj[..., :RWKV_COLS]
    za = za + lp['shift_mu'] * centred_shift(za)
    zb = proj[..., RWKV_COLS:RWKV_COLS + D_POOL]
    gates = jax.nn.sigmoid(proj[..., RWKV_COLS + D_POOL:])
    gate_a, gate_b = gates[..., :D_MODEL], gates[..., D_MODEL:]
    ya, S_fin = rwkv7_bidir(za, S0, lp)
    yb = pool_mixer(zb, rows, lp)
    merged = gate_a * (ya @ lp['w_up_a']) + gate_b * (yb @ lp['w_up_b'])
    x = x + g1 * (merged @ lp['w_out'])
    h2 = rms_norm(x, lp['norm2_g']) * (1.0 + sc2) + sh2
    u = h2 @ lp['ffn_w13']
    x = x + g2 * ((jax.nn.silu(u[..., :D_FF]) * u[..., D_FF:]) @ lp['ffn_w2'])
    return x, S_fin


def setup_inputs(seed: int = 0) -> dict:
    key = jax.random.key(seed)
    ks = jax.random.split(key, 32)

    def nrm(k, shape, scale):
        return jax.random.normal(k, shape, F32) * scale

    H, N = N_HEADS_RWKV, HEAD_SIZE
    return {
        'x_prompt': nrm(ks[0], (BATCH, SEQ, D_MODEL), 1.0),
        'x_sample': nrm(ks[1], (DEC_BATCH, DEC_SEQ, D_MODEL), 1.0),
        'c': nrm(ks[2], (DEC_BATCH, D_MODEL), 1.0),
        'state_rwkv': nrm(ks[3], (DEC_BATCH, DEPTH, 2, H, N, N), 0.5),
        'c_ctx': nrm(ks[4], (D_MODEL,), 1.0),
        'w_mod': nrm(ks[5], (DEPTH, D_MODEL, 6 * D_MODEL), 0.5 * D_MODEL ** -0.5),
        'b_mod': nrm(ks[6], (DEPTH, 6 * D_MODEL), 0.01),
        'norm1_g': 1.0 + nrm(ks[7], (DEPTH, D_MODEL), 0.02),
        'w_in': nrm(ks[8], (DEPTH, D_MODEL, IN_COLS), D_MODEL ** -0.5),
        'shift_mu': jax.random.uniform(ks[9], (DEPTH, RWKV_COLS), F32),
        'w0': nrm(ks[10], (DEPTH, 2, D_RWKV), 0.5),
        'w2': nrm(ks[11], (DEPTH, 2, DECAY_LORA, D_RWKV), DECAY_LORA ** -0.5),
        'a0': nrm(ks[12], (DEPTH, 2, D_RWKV), 0.5),
        'a2': nrm(ks[13], (DEPTH, 2, AAA_LORA, D_RWKV), AAA_LORA ** -0.5),
        'g2': nrm(ks[14], (DEPTH, GATE_LORA, D_RWKV), GATE_LORA ** -0.5),
        'k_k': 0.85 + nrm(ks[15], (DEPTH, D_RWKV), 0.05),
        'k_a': 1.0 + nrm(ks[16], (DEPTH, D_RWKV), 0.05),
        'r_k': nrm(ks[17], (DEPTH, H, N), 0.1),
        'lnx_g': 1.0 + nrm(ks[18], (DEPTH, D_RWKV), 0.02),
        'lnx_b': nrm(ks[19], (DEPTH, D_RWKV), 0.01),
        'w_up_a': nrm(ks[20], (DEPTH, D_RWKV, D_MODEL), D_RWKV ** -0.5),
        'pool_w': nrm(ks[21], (DEPTH, N_POOL_GROUPS, POOL_GROUP, POOL_GROUP), POOL_GROUP ** -0.5),
        'pool_scale': 1.0 + nrm(ks[22], (DEPTH, D_POOL), 0.02),
        'w_up_b': nrm(ks[23], (DEPTH, D_POOL, D_MODEL), D_POOL ** -0.5),
        'w_out': nrm(ks[24], (DEPTH, D_MODEL, D_MODEL), D_MODEL ** -0.5),
        'norm2_g': 1.0 + nrm(ks[25], (DEPTH, D_MODEL), 0.02),
        'ffn_w13': nrm(ks[26], (DEPTH, D_MODEL, 2 * D_FF), D_MODEL ** -0.5),
        'ffn_w2': nrm(ks[27], (DEPTH, D_FF, D_MODEL), D_FF ** -0.5),
        'final_g': 1.0 + nrm(ks[28], (D_MODEL,), 0.02),
    }


def reference(x_prompt, x_sample, c, state_rwkv, c_ctx, w_mod, b_mod, norm1_g, w_in, shift_mu,
              w0, w2, a0, a2, g2, k_k, k_a, r_k, lnx_g, lnx_b, w_up_a, pool_w, pool_scale,
              w_up_b, w_out, norm2_g, ffn_w13, ffn_w2, final_g):
    rows = x_sample.shape[1] // GRID_W
    xp, xs = x_prompt, x_sample
    S0_ctx = jnp.zeros((x_prompt.shape[0], 2, N_HEADS_RWKV, HEAD_SIZE, HEAD_SIZE), F32)
    new_states = []
    for l in range(DEPTH):
        lp = {
            'norm1_g': norm1_g[l], 'w_in': w_in[l], 'shift_mu': shift_mu[l],
            'w0': w0[l], 'w2': w2[l], 'a0': a0[l], 'a2': a2[l], 'g2': g2[l],
            'k_k': k_k[l], 'k_a': k_a[l], 'r_k': r_k[l], 'lnx_g': lnx_g[l], 'lnx_b': lnx_b[l],
            'w_up_a': w_up_a[l], 'pool_w': pool_w[l], 'pool_scale': pool_scale[l],
            'w_up_b': w_up_b[l], 'w_out': w_out[l], 'norm2_g': norm2_g[l],
            'ffn_w13': ffn_w13[l], 'ffn_w2': ffn_w2[l],
        }
        mod_ctx = jax.nn.silu(c_ctx)[None, :] @ w_mod[l] + b_mod[l]
        mod_lat = jax.nn.silu(c) @ w_mod[l] + b_mod[l]
        xp, S_ctx = trunk_layer(xp, mod_ctx, S0_ctx, None, lp)
        new_states.append(S_ctx)
        xs, _ = trunk_layer(xs, mod_lat, state_rwkv[:, l], rows, lp)
    y_prompt = rms_norm(xp, final_g)
    y_sample = rms_norm(xs, final_g)
    new_state_rwkv = jnp.stack(new_states, axis=1).astype(x_prompt.dtype)
    return (y_prompt, y_sample, new_state_rwkv)
```

```python
import os
import contextlib
import numpy as np
import ml_dtypes
import concourse.bass as bass
import concourse.mybir as mybir
from concourse.bass_utils import run_bass_kernel_spmd

F32 = mybir.dt.float32
BF16 = mybir.dt.bfloat16
ALU = mybir.AluOpType
AF = mybir.ActivationFunctionType
AX = mybir.AxisListType

D = 2048
DR = 1024
H = 16
N = 64
DFF = 5632
RW = 3456
INC = 8576
TS = 4096
TP = 256
NTOK = 4608
TT = 512
NT = NTOK // TT
EPS = 1e-6
GN_EPS = 64e-5
LD = 0.6065306597126334

ENGS = ("pe", "act", "dve", "pool", "sp")
NDS = 12


class Res:
    __slots__ = ("w", "r")

    def __init__(self):
        self.w = None
        self.r = []


def RL(n):
    return [Res() for _ in range(n)]


class Prog:
    def __init__(self, nc):
        self.nc = nc
        self.ops = {e: [] for e in ENGS}
        self.clock = {e: {} for e in ENGS}
        self.ndma = {e: 0 for e in ENGS}
        self.last = {e: None for e in ENGS}

    def _cand(self, eng, ev, c):
        if ev is None:
            return
        kind, f, i = ev
        if kind == 'e':
            if f == 'pe' and eng == 'pe':
                return
            key, val = ('e', f), i
        else:
            key, val = ('d', f, i % NDS), i // NDS
        if self.clock[eng].get(key, -1) >= val:
            return
        if c.get(key, -1) < val:
            c[key] = val

    def op(self, eng, fn, reads=(), writes=(), dma=False, explicit=False):
        c = {}
        for r in reads:
            self._cand(eng, r.w, c)
        for r in writes:
            self._cand(eng, r.w, c)
            for ev in r.r:
                self._cand(eng, ev, c)
        ev = self._raw(eng, fn, c, dma, explicit)
        for r in reads:
            r.r.append(ev)
        for r in writes:
            r.w = ev
            r.r = []
        return ev

    def _raw(self, eng, fn, c, dma=False, explicit=False):
        if dma:
            k = self.ndma[eng]
            if k >= NDS:
                self._cand(eng, ('d', eng, k - NDS), c)
        keys = list(c.keys())
        waits = {}
        for key in keys:
            val = c[key]
            implied = False
            for k2 in keys:
                if k2 == key or k2[0] != 'e':
                    continue
                ck2 = self.ops[k2[1]][c[k2]]['ck']
                if ck2.get(key, -1) >= val:
                    implied = True
                    break
            if not implied:
                waits[key] = val
        idx = len(self.ops[eng])
        if dma:
            k = self.ndma[eng]
            self.ndma[eng] += 1
            ev = ('d', eng, k)
        else:
            ev = ('e', eng, idx)
        ck = self.clock[eng]
        for key, val in c.items():
            if ck.get(key, -1) < val:
                ck[key] = val
            if key[0] == 'e':
                src = self.ops[key[1]][val]
                if key in waits:
                    src['sig'] = True
                for k2, v2 in src['ck'].items():
                    if ck.get(k2, -1) < v2:
                        ck[k2] = v2
        rec = dict(fn=fn, waits=waits, dma=dma, sig=False, ev=ev, ck=None,
                   explicit=explicit or eng == 'pe')
        if not dma:
            c2 = dict(ck)
            c2[('e', eng)] = idx
            rec['ck'] = c2
            self.last[eng] = ev
        self.ops[eng].append(rec)
        return ev

    def barrier(self):
        evs = {}
        for e in ENGS:
            c = {}
            n = self.ndma[e]
            for k in range(max(0, n - NDS), n):
                self._cand(e, ('d', e, k), c)
            if self.last[e] is not None:
                kind, f, i = self.last[e]
                key = ('e', f)
                if self.clock[e].get(key, -1) < i:
                    c[key] = i
            evs[e] = self._raw(e, lambda h: h.nop(), c, explicit=True)
        for e in ENGS:
            c = {}
            for f in ENGS:
                if f != e:
                    self._cand(e, evs[f], c)
            self._raw(e, lambda h: h.nop(), c, explicit=True)
            self.clock[e][('e', e)] = evs[e][2]

    def emit(self):
        nc = self.nc
        with contextlib.ExitStack() as st:
            esem = {e: st.enter_context(nc.semaphore("es_" + e)) for e in ENGS}
            dsem = {e: [st.enter_context(nc.semaphore("ds_%s_%d" % (e, j))) for j in range(NDS)]
                    for e in ENGS if self.ndma[e] > 0}
            signum = {}
            for e in ENGS:
                n = 0
                for i, rec in enumerate(self.ops[e]):
                    if rec['sig'] and not rec['dma']:
                        n += 1
                        signum[(e, i)] = n
            self.nsig = {e: sum(1 for r in self.ops[e] if r['sig']) for e in ENGS}
            block = st.enter_context(nc.Block())

            def run(e, h):
                for i, rec in enumerate(self.ops[e]):
                    wl = []
                    for key, val in rec['waits'].items():
                        if key[0] == 'e':
                            wl.append((esem[key[1]], signum[(key[1], val)]))
                        else:
                            wl.append((dsem[key[1]][key[2]], 16 * (val + 1)))
                    emb = None
                    if wl and not rec['explicit']:
                        emb = wl[0]
                        wl = wl[1:]
                    for (s, v) in wl:
                        h.wait_ge(s, v)
                    ins = rec['fn'](h)
                    if emb is not None:
                        ins._wait_ge(emb[0], emb[1])
                    if rec['dma']:
                        ins.then_inc(dsem[e][rec['ev'][2] % NDS], 16)
                    elif rec['sig']:
                        ins.then_inc(esem[e], 1)

            def final(h):
                for f in ENGS:
                    n = self.ndma[f]
                    for j in range(min(n, NDS)):
                        cnt = (n - 1 - j) // NDS + 1
                        h.wait_ge(dsem[f][j], 16 * cnt)

            @block.tensor
            def _(h):
                run("pe", h)

            @block.scalar
            def _(h):
                run("act", h)

            @block.vector
            def _(h):
                run("dve", h)

            @block.gpsimd
            def _(h):
                run("pool", h)

            @block.sync
            def _(h):
                run("sp", h)
                final(h)


VEC_LAYOUT = [("n1g", 16), ("n2g", 16), ("bm", 64), ("mu", 27), ("w0", 16), ("a0", 16), ("kk", 8),
              ("ka", 8), ("rk", 8), ("lng", 8), ("lnb", 8), ("psc", 8)]
VOFF = {}
_o = 0
for _n, _c in VEC_LAYOUT:
    VOFF[_n] = (_o, _c)
    _o += _c
NV = _o

ARENA_BYTES = 196 * 1024


class Builder:
    def __init__(self, debug=False, phases="PBC"):
        self.debug = debug
        self.phases = phases
        self.nc = bass.Bass("TRN2", target_bir_lowering=False)
        self.P = Prog(self.nc)
        self.st = contextlib.ExitStack()
        self.dr = {}
        self.rr = 0

    def din(self, name, shape, dt=F32):
        self.dr[name] = self.nc.dram_tensor(name, list(shape), dt, kind="ExternalInput").ap()
        return self.dr[name]

    def dout(self, name, shape, dt=F32):
        self.dr[name] = self.nc.dram_tensor(name, list(shape), dt, kind="ExternalOutput").ap()
        return self.dr[name]

    def dscr(self, name, shape, dt=F32, dbg=False):
        kind = "ExternalOutput" if (dbg and self.debug) else "Internal"
        self.dr[name] = self.nc.dram_tensor(name, list(shape), dt, kind=kind).ap()
        return self.dr[name]

    def I(self, eng, meth, reads=(), writes=(), **kw):
        dma = meth == "dma_start"
        return self.P.op(eng, lambda h: getattr(h, meth)(**kw), reads, writes, dma=dma,
                         explicit=("accum_out" in kw))

    def areset(self):
        self.aoff = 0

    def alloc(self, n, dt):
        sz = 4 if dt == F32 else 2
        self.aoff = (self.aoff + 63) // 64 * 64
        o = self.aoff
        self.aoff += n * sz
        assert self.aoff <= ARENA_BYTES, ("arena overflow", self.aoff)
        ap = self.arena[:, o // 2: o // 2 + n * sz // 2]
        return ap.bitcast(F32) if dt == F32 else ap

    def bank(self):
        i = self.rr % 8
        self.rr += 1
        return self.ps[i], self.psb[i], self.rps[i]

    def ev_eng(self):
        self.evc = getattr(self, "evc", 0) + 1
        return "dve" if self.evc % 2 else "act"

    def wstream_init(self, slots, tiles):
        self.ws_slots = slots
        self.ws_tiles = tiles
        self.ws_issued = 0
        self.ws_cur = 0

    def wget(self):
        ns = len(self.ws_slots)
        while self.ws_issued < len(self.ws_tiles) and self.ws_issued < self.ws_cur + ns:
            i = self.ws_issued
            ap, r = self.ws_slots[i % ns]
            W, k0, kcn, c0, ncols = self.ws_tiles[i]
            src = W.rearrange("(kc p) n -> p kc n", p=128)[:, k0:k0 + kcn, c0:c0 + ncols]
            dst = ap.rearrange("p (kc n) -> p kc n", n=512)[:, :kcn, :ncols]
            self.I("sp", "dma_start", writes=[r], out=dst, in_=src)
            self.ws_issued += 1
        i = self.ws_cur
        self.ws_cur += 1
        ap, r = self.ws_slots[i % ns]
        return ap.rearrange("p (kc n) -> p kc n", n=512), r

    def build(self):
        nc, I = self.nc, self.I
        st = self.st
        xall = self.din("xall", [NTOK, D])
        cT = self.din("cT", [128, 32])
        state = self.din("state", [2, H, N, N])
        vecs = self.din("vecs", [128, NV])
        bmg = self.din("bmg", [2, D])
        fing = self.din("fing", [1, D])
        w_mod = self.din("w_mod", [D, 6 * D])
        w_in = self.din("w_in", [D, INC])
        w2l = self.din("w2l", [2, 64, DR])
        a2l = self.din("a2l", [2, 64, DR])
        g2l = self.din("g2l", [128, DR])
        w_up_a = self.din("w_up_a", [DR, D])
        pool_w = self.din("pool_w", [4, 256, 256])
        w_up_b = self.din("w_up_b", [DR, D])
        w_out = self.din("w_out", [D, D])
        w13 = self.din("w13", [D, 2 * DFF])
        w2f = self.din("w2f", [DFF, D])
        identb = self.din("identb", [128, 128], BF16)
        identf = self.din("identf", [128, 128], F32)
        self.din("invc", [8, 4096], F32)
        self.din("bmask", [4, 128, 512], F32)
        self.din("cmask", [128, 512], F32)
        self.din("onesbd", [128, 128], F32)
        self.din("istf", [128, 64], F32)
        y = self.dout("y", [NTOK, D])
        ns_out = self.dout("ns", [2, 2, H, N, N])
        wb_in = self.dscr("wb_in", [D, INC], BF16)
        wb_upa = self.dscr("wb_upa", [DR, D], BF16)
        wb_upb = self.dscr("wb_upb", [DR, D], BF16)
        wb_out = self.dscr("wb_out", [D, D], BF16)
        wb_13 = self.dscr("wb_13", [D, 2 * DFF], BF16)
        wb_2 = self.dscr("wb_2", [DFF, D], BF16)
        za = self.dscr("za", [RW, NTOK], F32, dbg=True)
        zb = self.dscr("zb", [DR, NTOK], F32, dbg=True)
        ya = self.dscr("ya", [DR, NTOK], BF16, dbg=True)
        yb = self.dscr("yb", [DR, NTOK], BF16, dbg=True)
        dbg_mod = self.dscr("dbg_mod", [128, 64 * 2 + 16 * 4], F32, dbg=True)
        self.arena = st.enter_context(nc.sbuf_tensor("arena", [128, ARENA_BYTES // 2], BF16))
        self.ps = []
        self.psb = []
        for i in range(8):
            t = st.enter_context(nc.psum_tensor("ps%d" % i, [128, 512], F32))
            self.ps.append(t)
            self.psb.append(t[:].bitcast(BF16))
        self.rps = RL(8)
        cst = st.enter_context(nc.sbuf_tensor("cst", [128, 2048], F32))
        cstb = st.enter_context(nc.sbuf_tensor("cstb", [128, 1024], BF16))
        rc = Res()
        vv = cst[:, 0:NV]
        modT = cst[:, 256:256 + 128].rearrange("p (j v) -> p j v", v=2)
        A1 = cst[:, 384:384 + 32].rearrange("p (v j) -> p v j", v=2)
        A2 = cst[:, 416:416 + 32].rearrange("p (v j) -> p v j", v=2)
        scT = cst[:, 448:448 + 32].rearrange("p (k v) -> p k v", v=2)
        identB = cstb[:, 0:128]
        identF = cst[:, 512:640]
        djunk = cst[:, 640:768]
        rdj = Res()

        def V(name):
            o, c = VOFF[name]
            return vv[:, o:o + c]

        self.V = V
        I("sp", "dma_start", writes=[rc], out=vv, in_=vecs[:, :])
        I("sp", "dma_start", writes=[rc], out=cst[:, 448:480], in_=cT[:, :])
        I("sp", "dma_start", writes=[rc], out=identB, in_=identb[:, :])
        I("sp", "dma_start", writes=[rc], out=identF, in_=identf[:, :])
        I("act", "activation", reads=[rc], writes=[rc], out=cst[:, 448:480], in_=cst[:, 448:480], func=AF.Silu)

        self.areset()
        slots = [(self.alloc(16 * 512, BF16), Res()) for _ in range(3)]
        gb = [[self.alloc(D, F32) for g in range(2)] for v in range(2)]
        rgb = Res()
        scB = self.alloc(16 * 2 * 128, F32).rearrange("p (k v m) -> p k v m", k=16, v=2)
        rscB = Res()
        for v in range(2):
            I("dve", "tensor_copy", reads=[rc], writes=[rscB], out=scB[:, :, v, :],
              in_=scT[:, :, v:v + 1].broadcast_to([128, 16, 128]))
        for v in range(2):
            for g in range(2):
                I("sp", "dma_start", writes=[rgb], out=gb[v][g], in_=bmg[g:g + 1, :].broadcast_to([128, D]))
        wm3 = w_mod.rearrange("(kc p) n -> p kc n", p=128)
        fm_cols = [0, 2048, 6144, 8192]
        tiles = []
        for gi in range(4):
            for t in range(8):
                tiles.append(("fm", gi * 16 + t * 2, fm_cols[gi] + t * 256))
        for g, c0 in enumerate([4096, 10240]):
            for t in range(8):
                tiles.append(("gb", g, c0 + t * 256, t))
        for ti, tl in enumerate(tiles):
            ap, r = slots[ti % 3]
            wv = ap.bitcast(F32).rearrange("p (kc n) -> p kc n", n=256)
            I("sp", "dma_start", writes=[r], out=wv, in_=wm3[:, :, tl[2]:tl[2] + 256])
            for v in range(2):
                pg, _, rpg = self.bank()
                for kc in range(16):
                    I("pe", "matmul", reads=[r, rscB], writes=[rpg], out=pg[:, 0:256],
                      lhsT=scB[:, kc, v, :], rhs=wv[:, kc, :], start=(kc == 0), stop=(kc == 15))
                if tl[0] == "fm":
                    for ch in range(2):
                        j = tl[1] + ch
                        I("dve", "scalar_tensor_tensor", reads=[rpg, rc], writes=[rc, rdj], out=djunk,
                          in0=pg[:, ch * 128:(ch + 1) * 128], scalar=1.0, in1=identF, op0=ALU.mult, op1=ALU.mult,
                          accum_out=modT[:, j, v:v + 1])
                else:
                    g, c0, t = tl[1], tl[2], tl[3]
                    I("dve", "tensor_tensor", reads=[rpg, rgb], writes=[rgb],
                      out=gb[v][g][:, t * 256:(t + 1) * 256], in0=pg[:, 0:256],
                      in1=gb[v][g][:, t * 256:(t + 1) * 256], op=ALU.add)
        for v in range(2):
            I("dve", "tensor_tensor", reads=[rc], writes=[rc], out=modT[:, :, v], in0=modT[:, :, v],
              in1=V("bm"), op=ALU.add)
        for v in range(2):
            I("dve", "scalar_tensor_tensor", reads=[rc], writes=[rc], out=A1[:, v, :], in0=modT[:, 16:32, v],
              scalar=1.0, in1=V("n1g"), op0=ALU.add, op1=ALU.mult)
            I("dve", "scalar_tensor_tensor", reads=[rc], writes=[rc], out=A2[:, v, :], in0=modT[:, 48:64, v],
              scalar=1.0, in1=V("n2g"), op0=ALU.add, op1=ALU.mult)
        if self.debug:
            I("sp", "dma_start", reads=[rc], out=dbg_mod[:, :], in_=cst[:, 256:256 + 192])
        gbd = self.dscr("gbd", [4, 128, D], F32, dbg=True)
        for v in range(2):
            for g in range(2):
                I("sp", "dma_start", reads=[rgb], out=gbd[v * 2 + g], in_=gb[v][g])
        stg = [(self.alloc(2048, F32), self.alloc(2048, BF16), Res(), Res()) for _ in range(4)]
        jobs = []
        for (W, Wb, K, NN) in [(w_in, wb_in, D, INC), (w_up_a, wb_upa, DR, D), (w_up_b, wb_upb, DR, D),
                               (w_out, wb_out, D, D), (w13, wb_13, D, 2 * DFF), (w2f, wb_2, DFF, D)]:
            for kc in range(K // 128):
                bounds = [0, 2048, 4096, 4480, 6528, INC] if W is w_in else list(range(0, NN, 2048)) + [NN]
                for n0, n1 in zip(bounds[:-1], bounds[1:]):
                    jobs.append((W, Wb, kc, n0, n1))
        jobs_now = [j for j in jobs if j[0] is w_in and j[3] < 4480]
        self.bg_jobs = []
        for (W, Wb, kc, n0, n1) in jobs:
            if W is w_in and n0 < 4480:
                continue
            for c0 in range(n0, n1, 512):
                self.bg_jobs.append((W, Wb, kc, c0, min(n1, c0 + 512)))
        self.convert_jobs(jobs_now, stg)
        self.P.barrier()

        self.areset()
        xt = self.alloc(4 * D, F32).rearrange("p (s d) -> p s d", s=4)
        rxt = RL(4)
        xn = self.alloc(4 * D, BF16).rearrange("p (s d) -> p s d", s=4)
        rxn = RL(4)
        hT2 = [self.alloc(16 * TT, BF16).rearrange("p (k t) -> p k t", k=16) for _ in range(4)]
        rhT2 = [RL(16) for _ in range(4)]
        small = self.alloc(64, F32)
        rsm = Res()
        slots = [(self.alloc(16 * 512, BF16), Res()) for _ in range(3)]
        ostg = [(self.alloc(TT, F32), Res()) for _ in range(4)]
        pairs = [[0, 1, 2, 3], [4, 5, 6, 7], [8]]
        wt = []
        for _pr in pairs:
            for t in range(9):
                wt.append((wb_in, 0, 16, t * 512, min(512, 4480 - t * 512)))
        self.wstream_init(slots, wt)
        self.norm_ctx = dict(xn=xn, rxn=rxn, junk=None, rjunk=None, small=small, rsm=rsm, identB=identB, rc=rc)
        oi = 0
        for pr in pairs:
            for hi, it in enumerate(pr):
                v = 0 if it < 8 else 1
                tok0 = it * TT
                for s_ in range(4):
                    I("sp", "dma_start", writes=[rxt[s_]], out=xt[:, s_, :], in_=xall[tok0 + s_ * 128: tok0 + (s_ + 1) * 128, :])
                self.norm_T(xt, rxt, A1[:, v, :], lambda kc, v=v: modT[:, kc, v:v + 1], hT2[hi], rhT2[hi], EPS)
            for t in range(9):
                wv, rw = self.wget()
                nch = 4 if t < 8 else 3
                for hi, it in enumerate(pr):
                    tok0 = it * TT
                    for ch in range(nch):
                        j = t * 4 + ch
                        pb, _, rp = self.bank()
                        for kc in range(16):
                            I("pe", "matmul", reads=[rw, rhT2[hi][kc]], writes=[rp], out=pb[:, :],
                              lhsT=wv[:, kc, ch * 128:(ch + 1) * 128], rhs=hT2[hi][:, kc, :], start=(kc == 0), stop=(kc == 15))
                        og, rog = ostg[oi % 4]
                        oi += 1
                        e = self.ev_eng()
                        if e == "dve":
                            I("dve", "tensor_copy", reads=[rp], writes=[rog], out=og, in_=pb[:, :])
                        else:
                            I("act", "copy", reads=[rp], writes=[rog], out=og, in_=pb[:, :])
                        if j < 27:
                            dst = za[j * 128:(j + 1) * 128, tok0:tok0 + TT]
                        else:
                            dst = zb[(j - 27) * 128:(j - 26) * 128, tok0:tok0 + TT]
                        I("pool", "dma_start", reads=[rog], out=dst, in_=og)
        self.P.barrier()
        self.hd = dict(xall=xall, y=y, za=za, zb=zb, ya=ya, yb=yb, gbd=gbd, fing=fing, wb_in=wb_in, wb_upa=wb_upa,
                       wb_upb=wb_upb, wb_out=wb_out, wb_13=wb_13, wb_2=wb_2, modT=modT, A1=A1, A2=A2, identB=identB,
                       identF=identF, rc=rc, pool_w=pool_w, state=state, ns_out=ns_out, w2l=w2l, a2l=a2l, g2l=g2l,
                       cst=cst, cstb=cstb)
        if "P" in self.phases:
            self.phase_P()
            self.P.barrier()
        if "B" in self.phases:
            self.phase_B()
            self.P.barrier()
        if "C" in self.phases:
            self.phase_C()
        self.P.emit()
        return nc

    def wsum(self, eng, cur, rcur, bufs, bi, w, axis, L):
        I = self.I

        def sl(ap, a, b):
            return ap[:, :, a:b] if axis == 2 else ap[:, a:b, :]
        o, ro = bufs[bi]
        bi ^= 1
        I(eng, "tensor_tensor", reads=[rcur], writes=[ro], out=sl(o, 1, L), in0=sl(cur, 0, L - 1), in1=sl(cur, 1, L), op=ALU.add)
        cur, rcur = o, ro
        lo, sh, k = 1, 1, 2
        while k < w:
            o, ro = bufs[bi]
            bi ^= 1
            I(eng, "tensor_tensor", reads=[rcur], writes=[ro], out=sl(o, lo + sh, L - lo - sh),
              in0=sl(cur, lo, L - lo - 2 * sh), in1=sl(cur, lo + 2 * sh, L - lo), op=ALU.add)
            cur, rcur = o, ro
            lo += sh
            sh *= 2
            k *= 2
        return cur, rcur, bi

    def phase_P(self):
        I, hd, V = self.I, self.hd, self.V
        zb, yb, pool_w, rc = hd["zb"], hd["yb"], hd["pool_w"], hd["rc"]
        invc = self.dr["invc"]
        self.areset()
        GW = 80
        LW = 272
        raw = [[self.alloc(GW * GW, F32) for _ in range(3)] for _ in range(2)]
        rr_ = [RL(3) for _ in range(2)]
        ic = self.alloc(4096, F32)
        ric = Res()
        dT = [self.alloc(4096, BF16) for _ in range(2)]
        rdT = RL(2)
        pwf = self.alloc(512, F32).rearrange("p (k n) -> p k n", k=2)
        pwb = self.alloc(512, BF16).rearrange("p (k n) -> p k n", k=2)
        rpwf, rpwb = Res(), Res()
        ost = [(self.alloc(512, BF16), Res()) for _ in range(4)]
        ws = (2, 4, 8, 16)
        for kind in ("s", "p"):
            for k in range(2):
                I("dve" if k == 0 else "pool", "memset", writes=[rr_[k][0]], ap=raw[k][0], constant=0.0)
            if kind == "s":
                view = lambda ap: ap.rearrange("p (r c) -> p r c", c=GW)
                inner = lambda ap3: ap3[:, 8:72, 8:72]
                icv = ic.rearrange("p (r c) -> p r c", c=64)
                dview = lambda ap: ap.rearrange("p (r c) -> p r c", c=64)
                tok0, ntb = 0, 8
            else:
                view = lambda ap: ap[:, 0:2 * LW].rearrange("p (r c) -> p r c", c=LW)
                inner = lambda ap3: ap3[:, :, 8:264]
                icv = ic[:, 0:512].rearrange("p (r c) -> p r c", c=256)
                dview = lambda ap: ap[:, 0:512].rearrange("p (r c) -> p r c", c=256)
                tok0, ntb = 4096, 1
            for gi in range(4):
                w = ws[gi]
                if kind == "s":
                    I("sp", "dma_start", writes=[ric], out=ic, in_=invc[gi:gi + 1, 0:4096].broadcast_to([128, 4096]))
                else:
                    I("sp", "dma_start", writes=[ric], out=ic[:, 0:512], in_=invc[4 + gi:5 + gi, 0:512].broadcast_to([128, 512]))
                I("sp", "dma_start", writes=[rpwf], out=pwf, in_=pool_w[gi].rearrange("(k p) n -> p k n", p=128))
                I("act", "copy", reads=[rpwf], writes=[rpwb], out=pwb, in_=pwf)
                for k in range(2):
                    q = gi * 2 + k
                    eng = "dve" if k == 0 else "pool"
                    G3 = view(raw[k][0])
                    if kind == "s":
                        src = zb[q * 128:(q + 1) * 128, 0:4096].rearrange("p (r c) -> p r c", c=64)
                    else:
                        src = zb[q * 128:(q + 1) * 128, 4096:4608].rearrange("p (r c) -> p r c", c=256)
                    if kind == "s":
                        for r4 in range(4):
                            I("sp", "dma_start", writes=[rr_[k][0]], out=G3[:, 8 + 16 * r4:24 + 16 * r4, 8:72], in_=src[:, 16 * r4:16 * r4 + 16, :])
                    else:
                        I("sp", "dma_start", writes=[rr_[k][0]], out=inner(G3), in_=src)
                    bufs = [(view(raw[k][1]), rr_[k][1]), (view(raw[k][2]), rr_[k][2])]
                    cur, rcur, bi = self.wsum(eng, G3, rr_[k][0], bufs, 0, w, 2, GW if kind == "s" else LW)
                    if kind == "s":
                        cur, rcur, bi = self.wsum(eng, cur, rcur, bufs, bi, w, 1, GW)
                    o, ro = bufs[bi]
                    I(eng, "tensor_tensor", reads=[rcur, ric], writes=[ro], out=inner(o), in0=inner(cur), in1=icv, op=ALU.mult)
                    I(eng, "tensor_tensor", reads=[ro, rr_[k][0]], writes=[rdT[k]], out=dview(dT[k]), in0=inner(o),
                      in1=inner(G3), op=ALU.subtract)
                for tb_ in range(ntb):
                    for oc in range(2):
                        pb, _, rp = self.bank()
                        for kc in range(2):
                            I("pe", "matmul", reads=[rpwb, rdT[kc]], writes=[rp], out=pb[:, :],
                              lhsT=pwb[:, kc, oc * 128:(oc + 1) * 128], rhs=dT[kc][:, tb_ * 512:(tb_ + 1) * 512],
                              start=(kc == 0), stop=(kc == 1))
                        og, rog = ost[(tb_ * 2 + oc) % 4]
                        qq = gi * 2 + oc
                        I("act", "activation", reads=[rp, rc], writes=[rog], out=og, in_=pb[:, :], func=AF.Copy,
                          scale=V("psc")[:, qq:qq + 1])
                        I("pool", "dma_start", reads=[rog], out=yb[qq * 128:(qq + 1) * 128, tok0 + tb_ * 512: tok0 + (tb_ + 1) * 512],
                          in_=og)

    def phase_B(self):
        I, hd, V = self.I, self.hd, self.V
        za, ya, state, ns_out, rc, identB, identF = (hd[k] for k in ("za", "ya", "state", "ns_out", "rc", "identB", "identF"))
        yfs = self.dscr("yfs", [8, 128, NTOK // 64, 64], F32)
        ryfs = [RL(NTOK // 256) for _ in range(8)]
        self.areset()
        A = self.alloc
        NB = 256
        NCH = 4
        KU = 4
        mk = A(4 * 512, F32).rearrange("p (d x) -> p d x", d=4)
        cmask = A(512, F32)
        onesbd = A(128, F32)
        istf = A(64, F32)
        istb = A(64, BF16)
        lz = [A(DR, BF16) for _ in range(4)]
        g2b_ = A(DR, BF16)
        rk_ = Res()
        sv = A(64, F32)
        sv2 = A(16, F32)
        omu, hmu = sv[:, 0:27], sv[:, 27:54]
        oka, oka2 = sv2[:, 0:8], sv2[:, 8:16]
        Sst = A(8 * 64, F32).rearrange("p (q v) -> p q v", q=8)
        rS = RL(8)
        Sbd = (A(128, F32), Res())
        nsst = (A(64, F32), Res())

        def F(n=NB):
            return A(n, F32), Res()

        def Bf(n):
            return A(n, BF16), Res()

        def mkset():
            B = {}
            B["zin"] = [F(NB + 2) for _ in range(3)]
            for nm in ("rs", "ks", "vs", "t1", "sq", "rn", "kk", "sgw", "aa", "Ls", "Lx", "Lex", "E1", "E2", "E3", "t3", "kd", "t4"):
                B[nm] = F()
            for nm, al in (("ysum", "sq"), ("yc", "rn"), ("ysq", "Lex"), ("yfl", "sgw"), ("t6", "Ls"), ("t7", "Lx"), ("t8", "E1"), ("af", "E3")):
                B[nm] = B[al]
            B["gam"] = F(8)
            B["gst"] = F(32)
            for nm in ("KT", "BT", "VB"):
                B[nm] = Bf(NCH * 128)
            for nm in ("RA", "AK", "AB"):
                B[nm] = Bf(NCH * 256)

            def al(nm):
                return (B[nm][0].bitcast(BF16), B[nm][1])
            B["Ktm"], B["Btm"], B["N0T"] = al("sgw"), al("Ls"), al("Lex")
            B["Nb"] = [al("E1"), al("E2")]
            B["NTb"] = [al("E3"), al("t3")]
            B["Pb"] = [al("kd"), al("t4")]
            B["Vtm"] = Bf(NCH * 64)
            B["UT"] = Bf(NCH * 64)
            B["SBs"] = Bf((NCH + 1) * 64)
            B["Uraw"] = Bf(64)
            B["YN"] = (A(NCH * 128, F32), Res())
            B["ost"] = Bf(NB)
            return B

        sets = [mkset() for _ in range(KU)]
        for i_, B in enumerate(sets):
            B["banks"] = (2 * i_, 2 * i_ + 1)
            B["bk"] = 0
        shr = []
        for _ in range(2):
            shr.append(dict(zlo=[F(NB + 2) for _ in range(3)], tmp=F(), t1=F(), tw=Bf(NB), a_in=Bf(NB), sgd=Bf(NB)))
        lstage_ap, rls = sets[0]["YN"][0], sets[0]["YN"][1]
        lstage = lstage_ap
        I("sp", "dma_start", writes=[rk_], out=mk, in_=self.dr["bmask"].rearrange("d p x -> p d x"))
        I("sp", "dma_start", writes=[rk_], out=cmask, in_=self.dr["cmask"][:, :])
        I("sp", "dma_start", writes=[rk_], out=onesbd, in_=self.dr["onesbd"][:, :])
        I("sp", "dma_start", writes=[rk_], out=istf, in_=self.dr["istf"][:, :])
        I("act", "copy", reads=[rk_], writes=[rk_], out=istb, in_=istf)
        for i_, (src, d_) in enumerate([(hd["w2l"], 0), (hd["w2l"], 1), (hd["a2l"], 0), (hd["a2l"], 1)]):
            for hf in range(2):
                I("dve", "memset", writes=[rls], ap=lstage, constant=0.0)
                I("sp", "dma_start", writes=[rls], out=lstage[64 * d_:64 * d_ + 64, :], in_=src[d_][:, hf * 512:(hf + 1) * 512])
                I("dve", "tensor_copy", reads=[rls], writes=[rk_], out=lz[i_][:, hf * 512:(hf + 1) * 512], in_=lstage)
        for hf in range(2):
            I("sp", "dma_start", writes=[rls], out=lstage, in_=hd["g2l"][:, hf * 512:(hf + 1) * 512])
            I("dve", "tensor_copy", reads=[rls], writes=[rk_], out=g2b_[:, hf * 512:(hf + 1) * 512], in_=lstage)
        I("dve", "tensor_scalar", reads=[rc], writes=[rk_], out=omu, in0=V("mu"), scalar1=-1.0, scalar2=1.0, op0=ALU.mult, op1=ALU.add)
        I("dve", "tensor_scalar", reads=[rc], writes=[rk_], out=hmu, in0=V("mu"), scalar1=0.5, scalar2=None, op0=ALU.mult)
        I("dve", "tensor_scalar", reads=[rc], writes=[rk_], out=oka, in0=V("ka"), scalar1=-1.0, scalar2=1.0, op0=ALU.mult, op1=ALU.add)
        I("dve", "tensor_scalar", reads=[rc], writes=[rk_], out=oka2, in0=V("ka"), scalar1=-2.0, scalar2=2.0, op0=ALU.mult, op1=ALU.add)
        for B in sets:
            for nm in ("KT", "BT", "VB", "RA", "YN"):
                I("dve", "memset", writes=[B[nm][1]], ap=B[nm][0], constant=0.0)
        I("dve", "memset", writes=[Sbd[1]], ap=Sbd[0], constant=0.0)
        cch = self.dscr("cch", [4, 8, 128, NTOK], F32)
        rcch = [RL(NTOK // 256) for _ in range(8)]
        self.Bc = dict(cch=cch, rcch=rcch, za=za, ya=ya, state=state, ns_out=ns_out, rk_=rk_, identB=identB, identF=identF, mk=mk, cmask=cmask, onesbd=onesbd,
                       istf=istf, istb=istb, lz=lz, g2b_=g2b_, omu=omu, hmu=hmu, oka=oka, oka2=oka2, Sst=Sst, rS=rS, Sbd=Sbd, nsst=nsst,
                       yfs=yfs, ryfs=ryfs, shr=shr)
        seqs = [(0, TS), (TS, TP), (TS + TP, TP)]
        units = []
        tcount = 0
        for d_ in range(2):
            for si, (s0, slen) in enumerate(seqs):
                ntile = slen // NB
                torder = list(range(ntile)) if d_ == 0 else list(range(ntile - 1, -1, -1))
                for idx, ti in enumerate(torder):
                    for p in range(8):
                        units.append(dict(d=d_, si=si, s0=s0, s1=s0 + slen, tok0=s0 + ti * NB, p=p, first=(idx == 0),
                                          last=(idx == ntile - 1), sh=tcount % 2, lead=(p == 0)))
                    tcount += 1
        cstg = [(A(512, F32), A(512, BF16), Res(), Res()) for _ in range(3)]
        bg = self.bg_convert(self.bg_jobs, cstg)
        bg_live = True
        free = list(range(KU))
        active = []
        ui = 0
        rnd = 0
        last_start = -10 ** 9
        GAP = 0
        while ui < len(units) or active:
            if ui < len(units) and free and rnd - last_start >= GAP:
                bi = free.pop(0)
                active.append((self.rwkv_unit(units[ui], sets[bi]), bi))
                ui += 1
                last_start = rnd
            for item in list(active):
                g, bi = item
                try:
                    next(g)
                except StopIteration:
                    active.remove(item)
                    free.append(bi)
            rnd += 1
            if bg_live and rnd % 2 == 0:
                try:
                    next(bg)
                except StopIteration:
                    bg_live = False
        while bg_live:
            try:
                next(bg)
            except StopIteration:
                bg_live = False

    def bg_convert(self, jobs, stg):
        I = self.I
        n = len(jobs)
        ns = len(stg)
        for i in range(n + 2):
            k2 = i - 2
            if 0 <= k2 < n:
                W, Wb, kc, n0, n1 = jobs[k2]
                f, b, rf, rb = stg[k2 % ns]
                I("sp", "dma_start", reads=[rb], out=Wb[kc * 128:(kc + 1) * 128, n0:n1], in_=b[:, :n1 - n0])
            if i < n:
                W, Wb, kc, n0, n1 = jobs[i]
                f, b, rf, rb = stg[i % ns]
                I("sp", "dma_start", writes=[rf], out=f[:, :n1 - n0], in_=W[kc * 128:(kc + 1) * 128, n0:n1])
            k = i - 1
            if 0 <= k < n:
                W, Wb, kc, n0, n1 = jobs[k]
                f, b, rf, rb = stg[k % ns]
                if k % 2:
                    I("act", "copy", reads=[rf], writes=[rb], out=b[:, :n1 - n0], in_=f[:, :n1 - n0])
                else:
                    I("dve", "tensor_copy", reads=[rf], writes=[rb], out=b[:, :n1 - n0], in_=f[:, :n1 - n0])
            yield

    def rwkv_unit(self, u, B):
        I, V = self.I, self.V
        C = self.Bc
        za, ya, rk_, identB, identF, mk, cmask, onesbd, istf, istb, lz, g2b_, omu, hmu, oka, oka2 = (C[k] for k in (
            "za", "ya", "rk_", "identB", "identF", "mk", "cmask", "onesbd", "istf", "istb", "lz", "g2b_", "omu", "hmu", "oka", "oka2"))
        Sst, rS, Sbd, nsst, yfs, ryfs = (C[k] for k in ("Sst", "rS", "Sbd", "nsst", "yfs", "ryfs"))
        S = C["shr"][u["sh"]]
        rc = self.hd["rc"]
        d_, si, p, tok0, s0, s1 = u["d"], u["si"], u["p"], u["tok0"], u["s0"], u["s1"]
        n, nch = 256, 4
        cg0 = tok0 // 64
        zin, rs, ks, vs, t1, sq, rn, kk, sgw, aa, Ls, Lx, Lex, E1, E2, E3, t3, kd, t4, af = (B[k] for k in (
            "zin", "rs", "ks", "vs", "t1", "sq", "rn", "kk", "sgw", "aa", "Ls", "Lx", "Lex", "E1", "E2", "E3", "t3", "kd", "t4", "af"))
        gam, KT, BT, VB, RA, Ktm, Btm, AK, AB, N0T, Nb, NTb, Pb, Vtm, UT, SBs, Uraw = (B[k] for k in (
            "gam", "KT", "BT", "VB", "RA", "Ktm", "Btm", "AK", "AB", "N0T", "Nb", "NTb", "Pb", "Vtm", "UT", "SBs", "Uraw"))
        ysum, yc, ysq, yfl, YN, gst, t6, t7, t8, ost = (B[k] for k in ("ysum", "yc", "ysq", "yfl", "YN", "gst", "t6", "t7", "t8", "ost"))
        tw, a_in, sgd = S["tw"], S["a_in"], S["sgd"]

        def v3(ap, inner):
            return ap.rearrange("p (c x) -> p c x", x=inner)

        def ubank():
            i = B["banks"][B["bk"] % 2]
            B["bk"] += 1
            return self.ps[i], self.psb[i], self.rps[i]

        def shift(dst, rdst, src, rsrc, ch, tmp, eng2="pool"):
            I(eng2, "tensor_tensor", reads=[rsrc], writes=[tmp[1]], out=tmp[0][:, :n], in0=src[:, 0:n], in1=src[:, 2:n + 2], op=ALU.add)
            I("act", "activation", reads=[rsrc, rk_], writes=[rdst], out=dst[:, :n], in_=src[:, 1:n + 1], func=AF.Copy, scale=omu[:, ch:ch + 1])
            I("dve", "scalar_tensor_tensor", reads=[tmp[1], rdst, rk_], writes=[rdst], out=dst[:, :n], in0=tmp[0][:, :n],
              scalar=hmu[:, ch:ch + 1], in1=dst[:, :n], op0=ALU.mult, op1=ALU.add)

        def load_halo(buf, rbuf, row0):
            lo, hi = max(s0, tok0 - 1), min(s1, tok0 + n + 1)
            if tok0 - 1 < s0:
                I("pool", "memset", writes=[rbuf], ap=buf[:, 0:1], constant=0.0)
            if tok0 + n + 1 > s1:
                I("pool", "memset", writes=[rbuf], ap=buf[:, n + 1:n + 2], constant=0.0)
            I("sp", "dma_start", writes=[rbuf], out=buf[:, lo - (tok0 - 1): hi - (tok0 - 1)], in_=za[row0:row0 + 128, lo:hi])

        if u["lead"]:
            zlo, tmp, st1 = S["zlo"], S["tmp"], S["t1"]
            for j in range(3):
                load_halo(zlo[j][0], zlo[j][1], 3072 + j * 128)
            shift(tmp[0], tmp[1], zlo[0][0], zlo[0][1], 24, st1)
            I("act", "activation", reads=[tmp[1]], writes=[tw[1]], out=tw[0][:, :n], in_=tmp[0][:, :n], func=AF.Tanh)
            shift(tmp[0], tmp[1], zlo[1][0], zlo[1][1], 25, st1)
            I("act", "copy", reads=[tmp[1]], writes=[a_in[1]], out=a_in[0][:, :n], in_=tmp[0][:, :n])
            shift(tmp[0], tmp[1], zlo[2][0], zlo[2][1], 26, st1)
            I("act", "activation", reads=[tmp[1]], writes=[sgd[1]], out=sgd[0][:, :n], in_=tmp[0][:, :n], func=AF.Sigmoid)
        if u["first"]:
            if si == 0:
                for l in range(2):
                    I("sp", "dma_start", writes=[Sbd[1]], out=Sbd[0][64 * l:64 * l + 64, 64 * l:64 * l + 64], in_=C["state"][d_, 2 * p + l])
                pb, _, rp = ubank()
                I("pe", "matmul", reads=[Sbd[1], rk_], writes=[rp], out=pb[:, 0:64], lhsT=Sbd[0], rhs=istf, start=True, stop=True)
                I("dve", "tensor_copy", reads=[rp], writes=[rS[p]], out=Sst[:, p, :], in_=pb[:, 0:64])
            else:
                I("dve", "memset", writes=[rS[p]], ap=Sst[:, p, :], constant=0.0)
        cch = C["cch"]
        rcc = C["rcch"][p][tok0 // 256]
        if d_ == 0:
            for j, (dst, base) in enumerate([(rs, 0), (ks, 1024), (vs, 2048)]):
                load_halo(zin[j][0], zin[j][1], base + p * 128)
            yield
            for j, (dst, base) in enumerate([(rs, 0), (ks, 1024), (vs, 2048)]):
                shift(dst[0], dst[1], zin[j][0], zin[j][1], (base // 128) + p, t1)
                yield
            I("act", "activation", reads=[ks[1], rc], writes=[sq[1]], out=sq[0][:, :n], in_=ks[0][:, :n], func=AF.Square, scale=V("kk")[:, p:p + 1])
        else:
            for q_, dst in enumerate((rs, ks, vs, kk)):
                I("sp", "dma_start", reads=[rcc], writes=[dst[1]], out=dst[0][:, :n], in_=cch[q_, p, :, tok0:tok0 + n])
        pbw, _, rpw = ubank()
        I("pe", "matmul", reads=[tw[1], rk_], writes=[rpw], out=pbw[:, 0:n], lhsT=lz[d_][:, p * 128:(p + 1) * 128], rhs=tw[0][:, :n], start=True, stop=True)
        I("pe", "matmul", reads=[a_in[1], rk_], writes=[rpw], out=pbw[:, 256:256 + n], lhsT=lz[2 + d_][:, p * 128:(p + 1) * 128], rhs=a_in[0][:, :n], start=True, stop=True)
        if d_ == 0:
            pb, _, rp = ubank()
            I("pe", "matmul", reads=[sq[1], rk_], writes=[rp], out=pb[:, :n], lhsT=onesbd, rhs=sq[0][:, :n], start=True, stop=True)
        yield
        I("act", "activation", reads=[rpw, rc], writes=[sgw[1]], out=sgw[0][:, :n], in_=pbw[:, 0:n], func=AF.Sigmoid,
          bias=V("w0")[:, d_ * 8 + p:d_ * 8 + p + 1])
        I("act", "activation", reads=[rpw, rc], writes=[aa[1]], out=aa[0][:, :n], in_=pbw[:, 256:256 + n], func=AF.Sigmoid,
          bias=V("a0")[:, d_ * 8 + p:d_ * 8 + p + 1])
        yield
        if d_ == 0:
            I("act", "activation", reads=[rp], writes=[rn[1]], out=rn[0][:, :n], in_=pb[:, :n], func=AF.Sqrt)
            yield
        I("dve", "tensor_tensor_scan", reads=[sgw[1], rk_], writes=[Ls[1]], out=Ls[0][:, :n], data0=cmask[:, :n], data1=sgw[0][:, :n],
          initial=0.0, op0=ALU.mult, op1=ALU.add)
        yield
        if d_ == 0:
            I("dve", "tensor_scalar", reads=[rn[1]], writes=[rn[1]], out=rn[0][:, :n], in0=rn[0][:, :n], scalar1=1e-12, scalar2=None, op0=ALU.max)
            I("dve", "reciprocal", reads=[rn[1]], writes=[rn[1]], out=rn[0][:, :n], in_=rn[0][:, :n])
            yield
            I("dve", "scalar_tensor_tensor", reads=[ks[1], rn[1], rc], writes=[kk[1]], out=kk[0][:, :n], in0=ks[0][:, :n],
              scalar=V("kk")[:, p:p + 1], in1=rn[0][:, :n], op0=ALU.mult, op1=ALU.mult)
            yield
            for q_, src_ in enumerate((rs, ks, vs, kk)):
                I("sp", "dma_start", reads=[src_[1]], writes=[rcc], out=cch[q_, p, :, tok0:tok0 + n], in_=src_[0][:, :n])
        Ls3 = v3(Ls[0][:, :n], 64)
        if d_ == 0:
            LxT = Ls
        else:
            I("pool", "tensor_tensor", reads=[sgw[1], Ls[1]], writes=[Lx[1]], out=Lx[0][:, :n], in0=sgw[0][:, :n], in1=Ls[0][:, :n], op=ALU.subtract)
            I("dve", "tensor_tensor", reads=[Lx[1], Ls[1]], writes=[Lx[1]], out=v3(Lx[0][:, :n], 64), in0=v3(Lx[0][:, :n], 64),
              in1=Ls3[:, :, 63:64].broadcast_to([128, nch, 64]), op=ALU.add)
            LxT = Lx
            yield
        I("pool", "tensor_tensor", reads=[LxT[1], sgw[1]], writes=[Lex[1]], out=Lex[0][:, :n], in0=LxT[0][:, :n], in1=sgw[0][:, :n], op=ALU.subtract)
        I("act", "activation", reads=[LxT[1]], writes=[E1[1]], out=E1[0][:, :n], in_=LxT[0][:, :n], func=AF.Exp, scale=-LD)
        I("act", "activation", reads=[LxT[1]], writes=[E2[1]], out=E2[0][:, :n], in_=LxT[0][:, :n], func=AF.Exp, scale=LD)
        I("act", "activation", reads=[Ls[1]], writes=[gam[1]], out=gam[0][:, :nch], in_=Ls3[:, :, 63], func=AF.Exp, scale=-LD)
        I("act", "activation", reads=[aa[1], rc, rk_], writes=[t3[1]], out=t3[0][:, :n], in_=aa[0][:, :n], func=AF.Identity,
          scale=V("ka")[:, p:p + 1], bias=oka[:, p:p + 1])
        I("pool", "tensor_tensor", reads=[kk[1], aa[1]], writes=[t4[1]], out=t4[0][:, :n], in0=kk[0][:, :n], in1=aa[0][:, :n], op=ALU.mult)
        yield
        I("act", "activation", reads=[Lex[1]], writes=[E3[1]], out=E3[0][:, :n], in_=Lex[0][:, :n], func=AF.Exp, scale=-LD)
        I("pool", "tensor_tensor", reads=[t3[1], ks[1]], writes=[kd[1]], out=kd[0][:, :n], in0=t3[0][:, :n], in1=ks[0][:, :n], op=ALU.mult)
        yield
        KT3, BT3, VB3 = v3(KT[0], 128), v3(BT[0], 128), v3(VB[0], 128)
        RA4 = RA[0].rearrange("p (c t x) -> p c t x", t=2, x=128)
        for l in range(2):
            ps_, cs_ = slice(64 * l, 64 * l + 64), slice(64 * l, 64 * l + 64)
            e1, e2 = ("dve", "pool") if l == 0 else ("pool", "dve")
            I(e1, "tensor_tensor", reads=[kd[1], E2[1]], writes=[KT[1]], out=KT3[ps_, :nch, cs_], in0=v3(kd[0][ps_, :n], 64), in1=v3(E2[0][ps_, :n], 64), op=ALU.mult)
            I(e2, "tensor_tensor", reads=[t4[1], E2[1]], writes=[BT[1]], out=BT3[ps_, :nch, cs_], in0=v3(t4[0][ps_, :n], 64), in1=v3(E2[0][ps_, :n], 64), op=ALU.mult)
            yield
            I(e1, "tensor_tensor", reads=[rs[1], E1[1]], writes=[RA[1]], out=RA4[ps_, :nch, 0, cs_], in0=v3(rs[0][ps_, :n], 64), in1=v3(E1[0][ps_, :n], 64), op=ALU.mult)
            if l == 0:
                I("pool", "tensor_tensor", reads=[kk[1], E3[1]], writes=[E3[1]], out=E3[0][:, :n], in0=kk[0][:, :n], in1=E3[0][:, :n], op=ALU.mult)
            I("act", "activation", reads=[E3[1]], writes=[RA[1]], out=RA4[ps_, :nch, 1, cs_], in_=v3(E3[0][ps_, :n], 64), func=AF.Copy, scale=-1.0)
            I("act", "copy", reads=[vs[1]], writes=[VB[1]], out=VB3[ps_, :nch, cs_], in_=v3(vs[0][ps_, :n], 64))
            yield
        AK4 = AK[0].rearrange("p (c x) -> p c x", x=256)
        AB4 = AB[0].rearrange("p (c x) -> p c x", x=256)

        def a_mm(lh3, rl, c2):
            pb_, _, rp_ = ubank()
            for cc in range(2):
                c = c2 * 2 + cc
                I("pe", "matmul", reads=[rl, RA[1]], writes=[rp_], out=pb_[:, cc * 256:(cc + 1) * 256], lhsT=lh3[:, c, :],
                  rhs=RA4[:, c, :, :].rearrange("p t x -> p (t x)"), start=True, stop=True)
            return pb_, rp_

        def a_ev(pb_, rp_, dst4, rdst, c2):
            I("dve", "tensor_tensor", reads=[rp_, rk_], writes=[rdst], out=dst4[:, c2 * 2:c2 * 2 + 2, :].rearrange("p c x -> p (c x)"),
              in0=pb_[:, :], in1=mk[:, d_, :], op=ALU.mult)

        pbn, _, rpn = ubank()
        for c in range(nch):
            I("pe", "matmul", reads=[RA[1], BT[1]], writes=[rpn], out=pbn[:, c * 128:(c + 1) * 128], lhsT=RA4[:, c, 1, :], rhs=BT3[:, c, :], start=True, stop=True)
        pb0, rp0 = a_mm(BT3, BT[1], 0)
        yield
        I("dve", "tensor_tensor", reads=[rpn, rk_], writes=[N0T[1]], out=N0T[0][:, :nch * 128], in0=pbn[:, :nch * 128],
          in1=mk[:, 2 + d_, :nch * 128], op=ALU.mult)
        a_ev(pb0, rp0, AB4, AB[1], 0)
        pb1_, rp1_ = a_mm(BT3, BT[1], 1)
        _, pbk, rpk = ubank()
        for c in range(nch):
            I("pe", "transpose", reads=[KT[1], rc], writes=[rpk], out=pbk[:, c * 128:(c + 1) * 128], in_=KT3[:, c, :], identity=identB)
        yield
        a_ev(pb1_, rp1_, AB4, AB[1], 1)
        I("act", "copy", reads=[rpk], writes=[Ktm[1]], out=Ktm[0][:, :nch * 128], in_=pbk[:, :nch * 128])

        def d_btm():
            _, pbk_, rpk_ = ubank()
            for c in range(nch):
                I("pe", "transpose", reads=[BT[1], rc], writes=[rpk_], out=pbk_[:, c * 128:(c + 1) * 128], in_=BT3[:, c, :], identity=identB)
            return lambda: I("act", "copy", reads=[rpk_], writes=[Btm[1]], out=Btm[0][:, :nch * 128], in_=pbk_[:, :nch * 128])

        def d_vtm():
            pbv_, _, rpv_ = ubank()
            for c in range(nch):
                I("pe", "matmul", reads=[VB[1], rk_], writes=[rpv_], out=pbv_[:, c * 64:(c + 1) * 64], lhsT=VB3[:, c, :], rhs=istb, start=True, stop=True)
            return lambda: I("act", "copy", reads=[rpv_], writes=[Vtm[1]], out=Vtm[0][:, :nch * 64], in_=pbv_[:, :nch * 64])

        def d_ak(c2):
            pb_, rp_ = a_mm(KT3, KT[1], c2)
            return lambda: a_ev(pb_, rp_, AK4, AK[1], c2)

        deferred = [d_btm, d_vtm, lambda: d_ak(0), lambda: d_ak(1)]
        W_ = nch * 128
        P0 = Pb[0]
        I("pool", "tensor_tensor", reads=[AB[1], rc], writes=[P0[1]], out=v3(P0[0][:, :W_], 128), in0=AB4[:, :nch, 128:256],
          in1=identB.rearrange("p (o x) -> p o x", o=1).broadcast_to([128, nch, 128]), op=ALU.add)
        Ncur = (AB4[:, :nch, 128:256], AB[1])
        NTcur = (v3(N0T[0][:, :W_], 128), N0T[1])
        Pcur = P0
        pi = 0
        for j in range(1, 6):
            NTn = NTb[j % 2]
            Nn = Nb[j % 2]
            pb1, _, rp1 = ubank()
            for c in range(nch):
                I("pe", "matmul", reads=[Ncur[1], NTcur[1]], writes=[rp1], out=pb1[:, c * 128:(c + 1) * 128], lhsT=Ncur[0][:, c, :], rhs=NTcur[0][:, c, :], start=True, stop=True)
            if j < 5:
                pb2, _, rp2 = ubank()
                for c in range(nch):
                    I("pe", "matmul", reads=[Ncur[1], NTcur[1]], writes=[rp2], out=pb2[:, c * 128:(c + 1) * 128], lhsT=NTcur[0][:, c, :], rhs=Ncur[0][:, c, :], start=True, stop=True)
            yield
            I("act", "copy", reads=[rp1], writes=[NTn[1]], out=NTn[0][:, :W_], in_=pb1[:, :W_])
            if j < 5:
                I("act", "copy", reads=[rp2], writes=[Nn[1]], out=Nn[0][:, :W_], in_=pb2[:, :W_])
            NTcur = (v3(NTn[0][:, :W_], 128), NTn[1])
            if j < 5:
                Ncur = (v3(Nn[0][:, :W_], 128), Nn[1])
            Pn = Pb[(pi + 1) % 2]
            P3 = v3(Pcur[0][:, :W_], 128)
            pb3, _, rp3 = ubank()
            for c in range(nch):
                I("pe", "matmul", reads=[NTcur[1], Pcur[1]], writes=[rp3], out=pb3[:, c * 128:(c + 1) * 128], lhsT=NTcur[0][:, c, :], rhs=P3[:, c, :], start=True, stop=True)
            dev = deferred[j - 1]() if j <= 4 else None
            yield
            I("dve", "tensor_tensor", reads=[rp3, Pcur[1]], writes=[Pn[1]], out=Pn[0][:, :W_], in0=pb3[:, :W_], in1=Pcur[0][:, :W_], op=ALU.add)
            if dev is not None:
                dev()
            Pcur = Pn
            pi += 1
        Pf3 = v3(Pcur[0][:, :W_], 128)
        Vt3 = v3(Vtm[0][:, :nch * 64], 64)
        UT3 = v3(UT[0][:, :nch * 64], 64)
        SB3 = v3(SBs[0], 64)
        Kt3, Bt3 = v3(Ktm[0], 128), v3(Btm[0], 128)
        corder = list(range(nch)) if d_ == 0 else list(range(nch - 1, -1, -1))
        I("act", "copy", reads=[rS[p]], writes=[SBs[1]], out=SB3[:, corder[0], :], in_=Sst[:, p, :])
        yield
        for ci, c in enumerate(corder):
            pb, _, rp = ubank()
            I("pe", "matmul", reads=[RA[1], SBs[1]], writes=[rp], out=pb[:, 0:64], lhsT=RA4[:, c, 1, :], rhs=SB3[:, c, :], start=True, stop=False)
            I("pe", "matmul", reads=[AK[1], Vtm[1]], writes=[rp], out=pb[:, 0:64], lhsT=AK4[:, c, 128:256], rhs=Vt3[:, c, :], start=False, stop=True)
            yield
            I("act", "copy", reads=[rp], writes=[Uraw[1]], out=Uraw[0], in_=pb[:, 0:64])
            pb, _, rp = ubank()
            I("pe", "matmul", reads=[Pcur[1], Uraw[1]], writes=[rp], out=pb[:, 0:64], lhsT=Pf3[:, c, :], rhs=Uraw[0], start=True, stop=True)
            yield
            I("act", "copy", reads=[rp], writes=[UT[1]], out=UT3[:, c, :], in_=pb[:, 0:64])
            pb, _, rp = ubank()
            I("pe", "matmul", reads=[Ktm[1], Vtm[1]], writes=[rp], out=pb[:, 0:64], lhsT=Kt3[:, c, :], rhs=Vt3[:, c, :], start=True, stop=False)
            I("pe", "matmul", reads=[Btm[1], UT[1]], writes=[rp], out=pb[:, 0:64], lhsT=Bt3[:, c, :], rhs=UT3[:, c, :], start=False, stop=True)
            yield
            I("dve", "tensor_tensor", reads=[rp, rS[p]], writes=[rS[p]], out=Sst[:, p, :], in0=pb[:, 0:64], in1=Sst[:, p, :], op=ALU.add)
            if ci + 1 < nch:
                I("act", "activation", reads=[rS[p], gam[1]], writes=[SBs[1]], out=SB3[:, corder[ci + 1], :], in_=Sst[:, p, :], func=AF.Copy,
                  scale=gam[0][:, c:c + 1])
            I("act", "activation", reads=[rS[p], gam[1]], writes=[rS[p]], out=Sst[:, p, :], in_=Sst[:, p, :], func=AF.Copy, scale=gam[0][:, c:c + 1])
        if u["last"] and si > 0:
            for l in range(2):
                I("dve", "tensor_copy", reads=[rS[p]], writes=[Sbd[1]], out=Sbd[0][64 * l:64 * l + 64, 64 * l:64 * l + 64],
                  in_=Sst[64 * l:64 * l + 64, p, :])
            pb, _, rp = ubank()
            I("pe", "matmul", reads=[Sbd[1], rk_], writes=[rp], out=pb[:, 0:64], lhsT=Sbd[0], rhs=istf, start=True, stop=True)
            I("act", "copy", reads=[rp], writes=[nsst[1]], out=nsst[0], in_=pb[:, 0:64])
            I("sp", "dma_start", reads=[nsst[1]], out=C["ns_out"][si - 1, d_, 2 * p:2 * p + 2].rearrange("l v k -> (l v) k"), in_=nsst[0])
        pby, _, rpy = ubank()
        for c in range(nch):
            I("pe", "matmul", reads=[RA[1], SBs[1]], writes=[rpy], out=pby[:, c * 64:(c + 1) * 64], lhsT=RA4[:, c, 0, :], rhs=SB3[:, c, :], start=True, stop=False)
            I("pe", "matmul", reads=[AK[1], Vtm[1]], writes=[rpy], out=pby[:, c * 64:(c + 1) * 64], lhsT=AK4[:, c, 0:128], rhs=Vt3[:, c, :], start=False, stop=False)
            I("pe", "matmul", reads=[AB[1], UT[1]], writes=[rpy], out=pby[:, c * 64:(c + 1) * 64], lhsT=AB4[:, c, 0:128], rhs=UT3[:, c, :], start=False, stop=True)
        yslot = yfs[p, :, cg0:cg0 + nch, :]
        ry = ryfs[p][tok0 // 256]
        yield
        if d_ == 0:
            I("act", "copy", reads=[rpy], writes=[ysum[1]], out=ysum[0][:, :n], in_=pby[:, :n])
            yield
            I("sp", "dma_start", reads=[ysum[1]], writes=[ry], out=yslot, in_=v3(ysum[0][:, :n], 64))
            return
        I("sp", "dma_start", reads=[ry], writes=[yfl[1]], out=v3(yfl[0][:, :n], 64), in_=yslot)
        pbA, _, rpA = ubank()
        I("pe", "matmul", reads=[a_in[1], rk_], writes=[rpA], out=pbA[:, :n], lhsT=lz[2][:, p * 128:(p + 1) * 128], rhs=a_in[0][:, :n], start=True, stop=True)
        yield
        I("act", "activation", reads=[rpA, rc], writes=[af[1]], out=af[0][:, :n], in_=pbA[:, :n], func=AF.Sigmoid, bias=V("a0")[:, p:p + 1])
        I("dve", "tensor_tensor", reads=[rpy, yfl[1]], writes=[ysum[1]], out=ysum[0][:, :n], in0=pby[:, :n], in1=yfl[0][:, :n], op=ALU.add)
        yield
        ys3 = v3(ysum[0][:, :n], 64)
        yc3 = v3(yc[0][:, :n], 64)
        G_ = gst[0]
        I("dve", "tensor_reduce", reads=[ysum[1]], writes=[gst[1]], out=G_[:, 0:nch], in_=ys3, axis=AX.X, op=ALU.add)
        I("pool", "tensor_tensor", reads=[af[1], aa[1]], writes=[af[1]], out=af[0][:, :n], in0=af[0][:, :n], in1=aa[0][:, :n], op=ALU.add)
        yield
        I("dve", "tensor_scalar", reads=[gst[1]], writes=[gst[1]], out=G_[:, 8:8 + nch], in0=G_[:, 0:nch], scalar1=1.0 / 64, scalar2=None, op0=ALU.mult)
        I("dve", "tensor_scalar", reads=[af[1], rc, rk_], writes=[af[1]], out=af[0][:, :n], in0=af[0][:, :n], scalar1=V("ka")[:, p:p + 1],
          scalar2=oka2[:, p:p + 1], op0=ALU.mult, op1=ALU.add)
        yield
        I("dve", "tensor_tensor", reads=[ysum[1], gst[1]], writes=[yc[1]], out=yc3, in0=ys3,
          in1=G_[:, 8:8 + nch].rearrange("p (c o) -> p c o", o=1).broadcast_to([128, nch, 64]), op=ALU.subtract)
        I("pool", "tensor_tensor", reads=[af[1], ks[1]], writes=[af[1]], out=af[0][:, :n], in0=af[0][:, :n], in1=ks[0][:, :n], op=ALU.mult)
        yield
        I("pool", "tensor_tensor", reads=[yc[1]], writes=[ysq[1]], out=ysq[0][:, :n], in0=yc[0][:, :n], in1=yc[0][:, :n], op=ALU.mult)
        I("dve", "scalar_tensor_tensor", reads=[rs[1], af[1], rc], writes=[t7[1]], out=t7[0][:, :n], in0=rs[0][:, :n], scalar=V("rk")[:, p:p + 1],
          in1=af[0][:, :n], op0=ALU.mult, op1=ALU.mult)
        yield
        I("dve", "tensor_reduce", reads=[ysq[1]], writes=[gst[1]], out=G_[:, 16:16 + nch], in_=v3(ysq[0][:, :n], 64), axis=AX.X, op=ALU.add)
        pbB, _, rpB = ubank()
        I("pe", "matmul", reads=[t7[1], rk_], writes=[rpB], out=pbB[:, :n], lhsT=onesbd, rhs=t7[0][:, :n], start=True, stop=True)
        yield
        I("act", "activation", reads=[gst[1]], writes=[gst[1]], out=G_[:, 24:24 + nch], in_=G_[:, 16:16 + nch], func=AF.Sqrt, scale=1.0 / 64, bias=GN_EPS)
        I("dve", "tensor_tensor", reads=[rpB, vs[1]], writes=[t8[1]], out=t8[0][:, :n], in0=pbB[:, :n], in1=vs[0][:, :n], op=ALU.mult)
        yield
        I("dve", "reciprocal", reads=[gst[1]], writes=[gst[1]], out=G_[:, 24:24 + nch], in_=G_[:, 24:24 + nch])
        yield
        YN3 = v3(YN[0], 128)
        for l in range(2):
            ps_ = slice(64 * l, 64 * l + 64)
            I("dve" if l == 0 else "pool", "tensor_tensor", reads=[yc[1], gst[1]], writes=[YN[1]], out=YN3[ps_, :nch, ps_], in0=yc3[ps_],
              in1=G_[ps_, 24:24 + nch].rearrange("p (c o) -> p c o", o=1).broadcast_to([64, nch, 64]), op=ALU.mult)
        yield
        pbf, _, rpf = ubank()
        for c in range(nch):
            I("pe", "matmul", reads=[YN[1], rk_], writes=[rpf], out=pbf[:, c * 64:(c + 1) * 64], lhsT=YN3[:, c, :], rhs=istf, start=True, stop=True)
        yield
        I("act", "activation", reads=[rpf, rc], writes=[t6[1]], out=t6[0][:, :n], in_=pbf[:, :n], func=AF.Identity,
          scale=V("lng")[:, p:p + 1], bias=V("lnb")[:, p:p + 1])
        yield
        I("pool", "tensor_tensor", reads=[t8[1], t6[1]], writes=[t8[1]], out=t8[0][:, :n], in0=t8[0][:, :n], in1=t6[0][:, :n], op=ALU.add)
        pbG, _, rpG = ubank()
        I("pe", "matmul", reads=[sgd[1], rk_], writes=[rpG], out=pbG[:, :n], lhsT=g2b_[:, p * 128:(p + 1) * 128], rhs=sgd[0][:, :n], start=True, stop=True)
        yield
        og, rog = ost
        I("dve", "tensor_tensor", reads=[rpG, t8[1]], writes=[rog], out=og[:, :n], in0=pbG[:, :n], in1=t8[0][:, :n], op=ALU.mult)
        yield
        I("sp", "dma_start", reads=[rog], out=ya[p * 128:(p + 1) * 128, tok0:tok0 + n], in_=og[:, :n])

    def phase_C(self):
        I, hd, V = self.I, self.hd, self.V
        xall, y, ya, yb, gbd, fing = (hd[k] for k in ("xall", "y", "ya", "yb", "gbd", "fing"))
        modT, A1, A2, identB, rc = (hd[k] for k in ("modT", "A1", "A2", "identB", "rc"))
        self.areset()
        xt = self.alloc(4 * D, F32).rearrange("p (s d) -> p s d", s=4)
        rxt = RL(4)
        xnraw = self.alloc(4 * D, BF16)
        xn = xnraw.rearrange("p (s d) -> p s d", s=4)
        rxn = RL(4)
        stg = xnraw.bitcast(F32).rearrange("p (k t) -> p k t", k=8)
        hT = self.alloc(16 * TT, BF16).rearrange("p (k t) -> p k t", k=16)
        rhT = RL(16)
        small = self.alloc(64, F32)
        rsm = Res()
        slots = [(self.alloc(16 * 512, BF16), Res()) for _ in range(3)]
        U = self.alloc(48 * TT, BF16).rearrange("p (k t) -> p k t", k=48)
        rU = RL(48)
        g1b = self.alloc(D, F32)
        g2b = self.alloc(D, F32)
        fgb = self.alloc(D, F32)
        rg = Res()
        I("sp", "dma_start", writes=[rg], out=fgb, in_=fing[0:1, :].broadcast_to([128, D]))
        wt = []
        for it in range(NT):
            for g in range(4):
                wt.append((hd["wb_in"], 0, 16, 4480 + g * 512, 512))
                wt.append((hd["wb_upa"], 0, 8, g * 512, 512))
                wt.append((hd["wb_in"], 0, 16, 4480 + 2048 + g * 512, 512))
                wt.append((hd["wb_upb"], 0, 8, g * 512, 512))
            for n in range(4):
                wt.append((hd["wb_out"], 0, 16, n * 512, 512))
            for g in range(11):
                wt.append((hd["wb_13"], 0, 16, g * 512, 512))
                wt.append((hd["wb_13"], 0, 16, DFF + g * 512, 512))
            for n in range(4):
                for kg in range(3):
                    wt.append((hd["wb_2"], kg * 16, 16 if kg < 2 else 12, n * 512, 512))
        self.wstream_init(slots, wt)
        self.norm_ctx = dict(xn=xn, rxn=rxn, junk=None, rjunk=None, small=small, rsm=rsm, identB=identB, rc=rc)
        for it in range(NT):
            v = 0 if it < 8 else 1
            tok0 = it * TT
            if it in (0, 8):
                I("sp", "dma_start", writes=[rg], out=g1b, in_=gbd[v * 2 + 0])
                I("sp", "dma_start", writes=[rg], out=g2b, in_=gbd[v * 2 + 1])
            for s in range(4):
                I("sp", "dma_start", writes=[rxt[s]], out=xt[:, s, :], in_=xall[tok0 + s * 128: tok0 + (s + 1) * 128, :])
            self.norm_T(xt, rxt, A1[:, v, :], lambda kc, v=v: modT[:, kc, v:v + 1], hT, rhT, EPS)
            yaT, ybT = U[:, 16:24, :], U[:, 24:32, :]
            I("sp", "dma_start", writes=rU[16:24], out=yaT, in_=ya.rearrange("(k p) t -> p k t", p=128)[:, :, tok0:tok0 + TT])
            I("sp", "dma_start", writes=rU[24:32], out=ybT, in_=yb.rearrange("(k p) t -> p k t", p=128)[:, :, tok0:tok0 + TT])
            for g in range(4):
                for half, (ysl, nk) in enumerate([(16, 8), (24, 8)]):
                    wv, rw = self.wget()
                    for jj in range(4):
                        pb, _, rp = self.bank()
                        for kc in range(16):
                            I("pe", "matmul", reads=[rw, rhT[kc]], writes=[rp], out=pb[:, :],
                              lhsT=wv[:, kc, jj * 128:(jj + 1) * 128], rhs=hT[:, kc, :], start=(kc == 0), stop=(kc == 15))
                        k = half * 4 + jj
                        I("act", "activation", reads=[rp], writes=[rxn[k // 2]], out=stg[:, k, :], in_=pb[:, :], func=AF.Sigmoid)
                    wv, rw = self.wget()
                    for jj in range(4):
                        pb, _, rp = self.bank()
                        for kc in range(8):
                            I("pe", "matmul", reads=[rw, rU[ysl + kc]], writes=[rp], out=pb[:, :],
                              lhsT=wv[:, kc, jj * 128:(jj + 1) * 128], rhs=U[:, ysl + kc, :], start=(kc == 0), stop=(kc == 7))
                        k = half * 4 + jj
                        I("dve", "tensor_tensor", reads=[rp, rxn[k // 2]], writes=[rxn[k // 2]], out=stg[:, k, :], in0=pb[:, :],
                          in1=stg[:, k, :], op=ALU.mult)
                for jj in range(4):
                    j = g * 4 + jj
                    I("pool", "tensor_tensor", reads=[rxn[jj // 2], rxn[2 + jj // 2]], writes=[rU[j]], out=U[:, j, :],
                      in0=stg[:, jj, :], in1=stg[:, 4 + jj, :], op=ALU.add)
            self.tm_out(U, rU, [(16, 0)], 4, g1b, rg, xt, rxt, stg, rxn)
            self.norm_T(xt, rxt, A2[:, v, :], lambda kc, v=v: modT[:, 32 + kc, v:v + 1], hT, rhT, EPS)
            for g in range(11):
                wv, rw = self.wget()
                for jj in range(4):
                    pb, _, rp = self.bank()
                    for kc in range(16):
                        I("pe", "matmul", reads=[rw, rhT[kc]], writes=[rp], out=pb[:, :],
                          lhsT=wv[:, kc, jj * 128:(jj + 1) * 128], rhs=hT[:, kc, :], start=(kc == 0), stop=(kc == 15))
                    I("act", "activation", reads=[rp], writes=[rxn[jj // 2]], out=stg[:, jj, :], in_=pb[:, :], func=AF.Silu)
                wv, rw = self.wget()
                for jj in range(4):
                    pb, _, rp = self.bank()
                    for kc in range(16):
                        I("pe", "matmul", reads=[rw, rhT[kc]], writes=[rp], out=pb[:, :],
                          lhsT=wv[:, kc, jj * 128:(jj + 1) * 128], rhs=hT[:, kc, :], start=(kc == 0), stop=(kc == 15))
                    j = g * 4 + jj
                    I("dve", "tensor_tensor", reads=[rp, rxn[jj // 2]], writes=[rU[j]], out=U[:, j, :], in0=pb[:, :],
                      in1=stg[:, jj, :], op=ALU.mult)
            self.tm_out(U, rU, [(16, 0), (16, 16), (12, 32)], 4, g2b, rg, xt, rxt, stg, rxn)
            for s in range(4):
                I("act", "activation", reads=[rxt[s]], writes=[rxn[s], rsm], out=xn[:, s, :], in_=xt[:, s, :], func=AF.Square,
                  accum_out=small[:, 16 + s:17 + s])
            I("act", "activation", reads=[rsm], writes=[rsm], out=small[:, 20:24], in_=small[:, 16:20], func=AF.Sqrt,
              scale=1.0 / D, bias=EPS)
            I("dve", "reciprocal", reads=[rsm], writes=[rsm], out=small[:, 24:28], in_=small[:, 20:24])
            yst = U[:, 0:32, :].rearrange("p k t -> p (k t)").bitcast(F32).rearrange("p (s d) -> p s d", s=4)
            for s in range(4):
                I("dve", "scalar_tensor_tensor", reads=[rxt[s], rsm, rg], writes=rU[8 * s:8 * s + 8], out=yst[:, s, :], in0=xt[:, s, :],
                  scalar=small[:, 24 + s:25 + s], in1=fgb, op0=ALU.mult, op1=ALU.mult)
                I("pool", "dma_start", reads=rU[8 * s:8 * s + 8], out=y[tok0 + s * 128: tok0 + (s + 1) * 128, :], in_=yst[:, s, :])

    def tm_out(self, U, rU, kgroups, ncol, gbt, rg, xt, rxt, stg, rxn):
        I = self.I
        for n in range(ncol):
            banks = [self.bank() for _ in range(4)]
            ngr = len(kgroups)
            for gi_, (kcn, k0) in enumerate(kgroups):
                wv, rw = self.wget()
                for s in range(4):
                    pb, _, rp = banks[s]
                    for kc in range(kcn):
                        I("pe", "matmul", reads=[rw, rU[k0 + kc]], writes=[rp], out=pb[:, :],
                          lhsT=U[:, k0 + kc, s * 128:(s + 1) * 128], rhs=wv[:, kc, :],
                          start=(gi_ == 0 and kc == 0), stop=(gi_ == ngr - 1 and kc == kcn - 1))
            for s in range(4):
                pb, _, rp = banks[s]
                k = (n * 4 + s) % 8
                I("dve", "tensor_tensor", reads=[rp, rg, rxn[k // 2]], writes=[rxn[k // 2]], out=stg[:, k, :], in0=pb[:, :],
                  in1=gbt[:, n * 512:(n + 1) * 512], op=ALU.mult)
                I("pool", "tensor_tensor", reads=[rxn[k // 2], rxt[s]], writes=[rxt[s]], out=xt[:, s, n * 512:(n + 1) * 512],
                  in0=stg[:, k, :], in1=xt[:, s, n * 512:(n + 1) * 512], op=ALU.add)

    def convert_jobs(self, jobs, stg):
        I = self.I
        n = len(jobs)
        ns = len(stg)
        engs = ["dve", "act", "pool"]
        for i in range(n + 2):
            if i < n:
                W, Wb, kc, n0, n1 = jobs[i]
                f, b, rf, rb = stg[i % ns]
                I("sp", "dma_start", writes=[rf], out=f[:, :n1 - n0], in_=W[kc * 128:(kc + 1) * 128, n0:n1])
            k = i - 2
            if k >= 0:
                W, Wb, kc, n0, n1 = jobs[k]
                f, b, rf, rb = stg[k % ns]
                e = engs[k % 3]
                if e == "act":
                    I("act", "copy", reads=[rf], writes=[rb], out=b[:, :n1 - n0], in_=f[:, :n1 - n0])
                else:
                    I(e, "tensor_copy", reads=[rf], writes=[rb], out=b[:, :n1 - n0], in_=f[:, :n1 - n0])
                I("sp", "dma_start", reads=[rb], out=Wb[kc * 128:(kc + 1) * 128, n0:n1], in_=b[:, :n1 - n0])

    def norm_T(self, xt, rxt, A, Bf, hT, rhT, eps):
        I = self.I
        c = self.norm_ctx
        xn, rxn, junk, rjunk, small, rsm, identB, rc = (c[k] for k in ("xn", "rxn", "junk", "rjunk", "small", "rsm", "identB", "rc"))
        for s in range(4):
            I("act", "activation", reads=[rxt[s]], writes=[rxn[s], rsm], out=xn[:, s, :], in_=xt[:, s, :], func=AF.Square,
              accum_out=small[:, s:s + 1])
        I("act", "activation", reads=[rsm], writes=[rsm], out=small[:, 4:8], in_=small[:, 0:4], func=AF.Sqrt,
          scale=1.0 / D, bias=eps)
        I("dve", "reciprocal", reads=[rsm], writes=[rsm], out=small[:, 8:12], in_=small[:, 4:8])
        for s in range(4):
            if s % 2 == 0:
                I("dve", "tensor_scalar", reads=[rxt[s], rsm], writes=[rxn[s]], out=xn[:, s, :], in0=xt[:, s, :],
                  scalar1=small[:, 8 + s:9 + s], scalar2=None, op0=ALU.mult)
            else:
                I("act", "activation", reads=[rxt[s], rsm], writes=[rxn[s]], out=xn[:, s, :], in_=xt[:, s, :],
                  func=AF.Copy, scale=small[:, 8 + s:9 + s])
        for kc in range(16):
            _, pbb, rp = self.bank()
            for s in range(4):
                I("pe", "transpose", reads=[rxn[s]], writes=[rp], out=pbb[:, s * 128:(s + 1) * 128],
                  in_=xn[:, s, kc * 128:(kc + 1) * 128], identity=identB)
            if kc % 2 == 0:
                I("dve", "tensor_scalar", reads=[rp, rc], writes=[rhT[kc]], out=hT[:, kc, :], in0=pbb[:, 0:512],
                  scalar1=A[:, kc:kc + 1], scalar2=Bf(kc), op0=ALU.mult, op1=ALU.add)
            else:
                I("act", "activation", reads=[rp, rc], writes=[rhT[kc]], out=hT[:, kc, :], in_=pbb[:, 0:512],
                  func=AF.Identity, scale=A[:, kc:kc + 1], bias=Bf(kc))


def _chunkT(v, nchunk):
    return np.ascontiguousarray(np.asarray(v, np.float32).reshape(nchunk, 128).T)


def _invc():
    out = np.zeros((8, 4096), np.float32)
    for gi, w in enumerate((2, 4, 8, 16)):
        def cnt(n):
            t = np.arange(n)
            return (np.clip(t - w // 2 + w, 0, n) - np.clip(t - w // 2, 0, n)).astype(np.float32)
        c64 = cnt(64)
        out[gi] = (1.0 / (c64[:, None] * c64[None, :])).reshape(-1)
        c256 = 1.0 / cnt(256)
        out[4 + gi, 0:512] = np.concatenate([c256, c256])
    return out


def make_inputs(core, inp):
    i = core
    m = {}
    m["xall"] = np.ascontiguousarray(np.concatenate(
        [inp["x_sample"][i], inp["x_prompt"][2 * i], inp["x_prompt"][2 * i + 1]], axis=0), dtype=np.float32)
    cc = np.stack([inp["c"][i], inp["c_ctx"]], axis=-1)
    m["cT"] = np.ascontiguousarray(cc.reshape(16, 128, 2).transpose(1, 0, 2).reshape(128, 32), dtype=np.float32)
    m["state"] = np.ascontiguousarray(inp["state_rwkv"][i, 0], dtype=np.float32)
    bm = inp["b_mod"][0]
    vec = {
        "n1g": _chunkT(inp["norm1_g"][0], 16), "n2g": _chunkT(inp["norm2_g"][0], 16),
        "bm": np.concatenate([_chunkT(bm[0:2048], 16), _chunkT(bm[2048:4096], 16),
                              _chunkT(bm[6144:8192], 16), _chunkT(bm[8192:10240], 16)], axis=1),
        "mu": _chunkT(inp["shift_mu"][0], 27),
        "w0": np.concatenate([_chunkT(inp["w0"][0, 0], 8), _chunkT(inp["w0"][0, 1], 8)], axis=1),
        "a0": np.concatenate([_chunkT(inp["a0"][0, 0], 8), _chunkT(inp["a0"][0, 1], 8)], axis=1),
        "kk": _chunkT(inp["k_k"][0], 8), "ka": _chunkT(inp["k_a"][0], 8),
        "rk": _chunkT(inp["r_k"][0].reshape(-1), 8), "lng": _chunkT(inp["lnx_g"][0], 8),
        "lnb": _chunkT(inp["lnx_b"][0], 8), "psc": _chunkT(inp["pool_scale"][0], 8),
    }
    m["vecs"] = np.ascontiguousarray(np.concatenate([vec[n] for n, _ in VEC_LAYOUT], axis=1), dtype=np.float32)
    m["bmg"] = np.ascontiguousarray(np.stack([bm[4096:6144], bm[10240:12288]]), dtype=np.float32)
    m["fing"] = np.ascontiguousarray(inp["final_g"][None, :], dtype=np.float32)
    m["w_mod"] = inp["w_mod"][0]
    m["w_in"] = inp["w_in"][0]
    m["w2l"] = inp["w2"][0]
    m["a2l"] = inp["a2"][0]
    m["g2l"] = inp["g2"][0]
    m["w_up_a"] = inp["w_up_a"][0]
    m["pool_w"] = inp["pool_w"][0]
    m["w_up_b"] = inp["w_up_b"][0]
    m["w_out"] = inp["w_out"][0]
    m["w13"] = inp["ffn_w13"][0]
    m["w2f"] = inp["ffn_w2"][0]
    m["identb"] = np.eye(128, dtype=np.float32).astype(ml_dtypes.bfloat16)
    m["identf"] = np.eye(128, dtype=np.float32)
    m["invc"] = _invc()
    s_ = np.arange(128)[:, None] % 64
    t_ = np.arange(128)[None, :] % 64
    blk = (np.arange(128)[:, None] // 64) == (np.arange(128)[None, :] // 64)
    up_i = ((s_ <= t_) & blk).astype(np.float32)
    up_s = ((s_ < t_) & blk).astype(np.float32)
    lo_i = ((s_ >= t_) & blk).astype(np.float32)
    lo_s = ((s_ > t_) & blk).astype(np.float32)
    m["bmask"] = np.ascontiguousarray(np.stack([
        np.concatenate([up_i, up_s, up_i, up_s], axis=1), np.concatenate([lo_i, lo_s, lo_i, lo_s], axis=1),
        np.concatenate([lo_s] * 4, axis=1), np.concatenate([up_s] * 4, axis=1)]), dtype=np.float32)
    cm = np.ones((128, 512), np.float32)
    cm[:, 0::64] = 0.0
    m["cmask"] = cm
    m["onesbd"] = blk.astype(np.float32)
    m["istf"] = np.ascontiguousarray(np.concatenate([np.eye(64), np.eye(64)], axis=0), dtype=np.float32)
    return m


def kernel(**inputs):
    inp = {k: np.asarray(v) for k, v in inputs.items()}
    debug = bool(os.environ.get("MK_DEBUG"))
    b = Builder(debug=debug, phases=os.environ.get("MK_PHASES", "PBC"))
    nc = b.build()
    in_maps = [make_inputs(i, inp) for i in range(8)]
    res = run_bass_kernel_spmd(nc, in_maps, core_ids=list(range(8)))
    if debug:
        return res
    ys = np.stack([r["y"][:TS] for r in res.results])
    yp = np.concatenate([r["y"][TS:].reshape(2, TP, D) for r in res.results], axis=0)
    nst = np.concatenate([r["ns"] for r in res.results], axis=0)[:, None]
    return (yp.astype(np.float32), ys.astype(np.float32), nst.astype(np.float32))
```

```python
import os
import contextlib
import numpy as np
import ml_dtypes
import concourse.bass as bass
import concourse.mybir as mybir
from concourse.bass_utils import run_bass_kernel_spmd

F32 = mybir.dt.float32
BF16 = mybir.dt.bfloat16
ALU = mybir.AluOpType
AF = mybir.ActivationFunctionType
AX = mybir.AxisListType

D = 2048
DR = 1024
H = 16
N = 64
DFF = 5632
RW = 3456
INC = 8576
TS = 4096
TP = 256
NTOK = 4608
TT = 512
NT = NTOK // TT
EPS = 1e-6
GN_EPS = 64e-5
LD = 0.6065306597126334

ENGS = ("pe", "act", "dve", "pool", "sp")
NDS = 12


class Res:
    __slots__ = ("w", "r")

    def __init__(self):
        self.w = None
        self.r = []


def RL(n):
    return [Res() for _ in range(n)]


class Prog:
    def __init__(self, nc):
        self.nc = nc
        self.ops = {e: [] for e in ENGS}
        self.clock = {e: {} for e in ENGS}
        self.ndma = {e: 0 for e in ENGS}
        self.last = {e: None for e in ENGS}

    def _cand(self, eng, ev, c):
        if ev is None:
            return
        kind, f, i = ev
        if kind == 'e':
            if f == 'pe' and eng == 'pe':
                return
            key, val = ('e', f), i
        else:
            key, val = ('d', f, i % NDS), i // NDS
        if self.clock[eng].get(key, -1) >= val:
            return
        if c.get(key, -1) < val:
            c[key] = val

    def op(self, eng, fn, reads=(), writes=(), dma=False, explicit=False):
        c = {}
        for r in reads:
            self._cand(eng, r.w, c)
        for r in writes:
            self._cand(eng, r.w, c)
            for ev in r.r:
                self._cand(eng, ev, c)
        ev = self._raw(eng, fn, c, dma, explicit)
        for r in reads:
            r.r.append(ev)
        for r in writes:
            r.w = ev
            r.r = []
        return ev

    def _raw(self, eng, fn, c, dma=False, explicit=False):
        if dma:
            k = self.ndma[eng]
            if k >= NDS:
                self._cand(eng, ('d', eng, k - NDS), c)
        keys = list(c.keys())
        waits = {}
        for key in keys:
            val = c[key]
            implied = False
            for k2 in keys:
                if k2 == key or k2[0] != 'e':
                    continue
                ck2 = self.ops[k2[1]][c[k2]]['ck']
                if ck2.get(key, -1) >= val:
                    implied = True
                    break
            if not implied:
                waits[key] = val
        idx = len(self.ops[eng])
        if dma:
            k = self.ndma[eng]
            self.ndma[eng] += 1
            ev = ('d', eng, k)
        else:
            ev = ('e', eng, idx)
        ck = self.clock[eng]
        for key, val in c.items():
            if ck.get(key, -1) < val:
                ck[key] = val
            if key[0] == 'e':
                src = self.ops[key[1]][val]
                if key in waits:
                    src['sig'] = True
                for k2, v2 in src['ck'].items():
                    if ck.get(k2, -1) < v2:
                        ck[k2] = v2
        rec = dict(fn=fn, waits=waits, dma=dma, sig=False, ev=ev, ck=None,
                   explicit=explicit or eng == 'pe')
        if not dma:
            c2 = dict(ck)
            c2[('e', eng)] = idx
            rec['ck'] = c2
            self.last[eng] = ev
        self.ops[eng].append(rec)
        return ev

    def barrier(self):
        evs = {}
        for e in ENGS:
            c = {}
            n = self.ndma[e]
            for k in range(max(0, n - NDS), n):
                self._cand(e, ('d', e, k), c)
            if self.last[e] is not None:
                kind, f, i = self.last[e]
                key = ('e', f)
                if self.clock[e].get(key, -1) < i:
                    c[key] = i
            evs[e] = self._raw(e, lambda h: h.nop(), c, explicit=True)
        for e in ENGS:
            c = {}
            for f in ENGS:
                if f != e:
                    self._cand(e, evs[f], c)
            self._raw(e, lambda h: h.nop(), c, explicit=True)
            self.clock[e][('e', e)] = evs[e][2]

    def emit(self):
        nc = self.nc
        with contextlib.ExitStack() as st:
            esem = {e: st.enter_context(nc.semaphore("es_" + e)) for e in ENGS}
            dsem = {e: [st.enter_context(nc.semaphore("ds_%s_%d" % (e, j))) for j in range(NDS)]
                    for e in ENGS if self.ndma[e] > 0}
            signum = {}
            for e in ENGS:
                n = 0
                for i, rec in enumerate(self.ops[e]):
                    if rec['sig'] and not rec['dma']:
                        n += 1
                        signum[(e, i)] = n
            self.nsig = {e: sum(1 for r in self.ops[e] if r['sig']) for e in ENGS}
            block = st.enter_context(nc.Block())

            def run(e, h):
                for i, rec in enumerate(self.ops[e]):
                    wl = []
                    for key, val in rec['waits'].items():
                        if key[0] == 'e':
                            wl.append((esem[key[1]], signum[(key[1], val)]))
                        else:
                            wl.append((dsem[key[1]][key[2]], 16 * (val + 1)))
                    emb = None
                    if wl and not rec['explicit']:
                        emb = wl[0]
                        wl = wl[1:]
                    for (s, v) in wl:
                        h.wait_ge(s, v)
                    ins = rec['fn'](h)
                    if emb is not None:
                        ins._wait_ge(emb[0], emb[1])
                    if rec['dma']:
                        ins.then_inc(dsem[e][rec['ev'][2] % NDS], 16)
                    elif rec['sig']:
                        ins.then_inc(esem[e], 1)

            def final(h):
                for f in ENGS:
                    n = self.ndma[f]
                    for j in range(min(n, NDS)):
                        cnt = (n - 1 - j) // NDS + 1
                        h.wait_ge(dsem[f][j], 16 * cnt)

            @block.tensor
            def _(h):
                run("pe", h)

            @block.scalar
            def _(h):
                run("act", h)

            @block.vector
            def _(h):
                run("dve", h)

            @block.gpsimd
            def _(h):
                run("pool", h)

            @block.sync
            def _(h):
                run("sp", h)
                final(h)


VEC_LAYOUT = [("n1g", 16), ("n2g", 16), ("bm", 64), ("mu", 27), ("w0", 16), ("a0", 16), ("kk", 8),
              ("ka", 8), ("rk", 8), ("lng", 8), ("lnb", 8), ("psc", 8)]
VOFF = {}
_o = 0
for _n, _c in VEC_LAYOUT:
    VOFF[_n] = (_o, _c)
    _o += _c
NV = _o

ARENA_BYTES = 196 * 1024


class Builder:
    def __init__(self, debug=False, phases="PBC"):
        self.debug = debug
        self.phases = phases
        self.nc = bass.Bass("TRN2", target_bir_lowering=False)
        self.P = Prog(self.nc)
        self.st = contextlib.ExitStack()
        self.dr = {}
        self.rr = 0

    def din(self, name, shape, dt=F32):
        self.dr[name] = self.nc.dram_tensor(name, list(shape), dt, kind="ExternalInput").ap()
        return self.dr[name]

    def dout(self, name, shape, dt=F32):
        self.dr[name] = self.nc.dram_tensor(name, list(shape), dt, kind="ExternalOutput").ap()
        return self.dr[name]

    def dscr(self, name, shape, dt=F32, dbg=False):
        kind = "ExternalOutput" if (dbg and self.debug) else "Internal"
        self.dr[name] = self.nc.dram_tensor(name, list(shape), dt, kind=kind).ap()
        return self.dr[name]

    def I(self, eng, meth, reads=(), writes=(), **kw):
        dma = meth == "dma_start"
        return self.P.op(eng, lambda h: getattr(h, meth)(**kw), reads, writes, dma=dma,
                         explicit=("accum_out" in kw))

    def areset(self):
        self.aoff = 0

    def alloc(self, n, dt):
        sz = 4 if dt == F32 else 2
        self.aoff = (self.aoff + 63) // 64 * 64
        o = self.aoff
        self.aoff += n * sz
        assert self.aoff <= ARENA_BYTES, ("arena overflow", self.aoff)
        ap = self.arena[:, o // 2: o // 2 + n * sz // 2]
        return ap.bitcast(F32) if dt == F32 else ap

    def bank(self):
        i = self.rr % 8
        self.rr += 1
        return self.ps[i], self.psb[i], self.rps[i]

    def ev_eng(self):
        self.evc = getattr(self, "evc", 0) + 1
        return "dve" if self.evc % 2 else "act"

    def wstream_init(self, slots, tiles):
        self.ws_slots = slots
        self.ws_tiles = tiles
        self.ws_issued = 0
        self.ws_cur = 0

    def wget(self):
        ns = len(self.ws_slots)
        while self.ws_issued < len(self.ws_tiles) and self.ws_issued < self.ws_cur + ns:
            i = self.ws_issued
            ap, r = self.ws_slots[i % ns]
            W, k0, kcn, c0, ncols = self.ws_tiles[i]
            src = W.rearrange("(kc p) n -> p kc n", p=128)[:, k0:k0 + kcn, c0:c0 + ncols]
            dst = ap.rearrange("p (kc n) -> p kc n", n=512)[:, :kcn, :ncols]
            self.I("sp", "dma_start", writes=[r], out=dst, in_=src)
            self.ws_issued += 1
        i = self.ws_cur
        self.ws_cur += 1
        ap, r = self.ws_slots[i % ns]
        return ap.rearrange("p (kc n) -> p kc n", n=512), r

    def build(self):
        nc, I = self.nc, self.I
        st = self.st
        xall = self.din("xall", [NTOK, D])
        cT = self.din("cT", [128, 32])
        state = self.din("state", [2, H, N, N])
        vecs = self.din("vecs", [128, NV])
        bmg = self.din("bmg", [2, D])
        fing = self.din("fing", [1, D])
        w_mod = self.din("w_mod", [D, 6 * D])
        w_in = self.din("w_in", [D, INC])
        w2l = self.din("w2l", [2, 64, DR])
        a2l = self.din("a2l", [2, 64, DR])
        g2l = self.din("g2l", [128, DR])
        w_up_a = self.din("w_up_a", [DR, D])
        pool_w = self.din("pool_w", [4, 256, 256])
        w_up_b = self.din("w_up_b", [DR, D])
        w_out = self.din("w_out", [D, D])
        w13 = self.din("w13", [D, 2 * DFF])
        w2f = self.din("w2f", [DFF, D])
        identb = self.din("identb", [128, 128], BF16)
        identf = self.din("identf", [128, 128], F32)
        self.din("invc", [8, 4096], F32)
        self.din("bmask", [4, 128, 512], F32)
        self.din("cmask", [128, 512], F32)
        self.din("onesbd", [128, 128], F32)
        self.din("istf", [128, 64], F32)
        y = self.dout("y", [NTOK, D])
        ns_out = self.dout("ns", [2, 2, H, N, N])
        wb_in = self.dscr("wb_in", [D, INC], BF16)
        wb_upa = self.dscr("wb_upa", [DR, D], BF16)
        wb_upb = self.dscr("wb_upb", [DR, D], BF16)
        wb_out = self.dscr("wb_out", [D, D], BF16)
        wb_13 = self.dscr("wb_13", [D, 2 * DFF], BF16)
        wb_2 = self.dscr("wb_2", [DFF, D], BF16)
        za = self.dscr("za", [RW, NTOK], F32, dbg=True)
        zb = self.dscr("zb", [DR, NTOK], F32, dbg=True)
        ya = self.dscr("ya", [DR, NTOK], BF16, dbg=True)
        yb = self.dscr("yb", [DR, NTOK], BF16, dbg=True)
        dbg_mod = self.dscr("dbg_mod", [128, 64 * 2 + 16 * 4], F32, dbg=True)
        self.arena = st.enter_context(nc.sbuf_tensor("arena", [128, ARENA_BYTES // 2], BF16))
        self.ps = []
        self.psb = []
        for i in range(8):
            t = st.enter_context(nc.psum_tensor("ps%d" % i, [128, 512], F32))
            self.ps.append(t)
            self.psb.append(t[:].bitcast(BF16))
        self.rps = RL(8)
        cst = st.enter_context(nc.sbuf_tensor("cst", [128, 2048], F32))
        cstb = st.enter_context(nc.sbuf_tensor("cstb", [128, 1024], BF16))
        rc = Res()
        vv = cst[:, 0:NV]
        modT = cst[:, 256:256 + 128].rearrange("p (j v) -> p j v", v=2)
        A1 = cst[:, 384:384 + 32].rearrange("p (v j) -> p v j", v=2)
        A2 = cst[:, 416:416 + 32].rearrange("p (v j) -> p v j", v=2)
        scT = cst[:, 448:448 + 32].rearrange("p (k v) -> p k v", v=2)
        identB = cstb[:, 0:128]
        identF = cst[:, 512:640]
        djunk = cst[:, 640:768]
        rdj = Res()

        def V(name):
            o, c = VOFF[name]
            return vv[:, o:o + c]

        self.V = V
        I("sp", "dma_start", writes=[rc], out=vv, in_=vecs[:, :])
        I("sp", "dma_start", writes=[rc], out=cst[:, 448:480], in_=cT[:, :])
        I("sp", "dma_start", writes=[rc], out=identB, in_=identb[:, :])
        I("sp", "dma_start", writes=[rc], out=identF, in_=identf[:, :])
        I("act", "activation", reads=[rc], writes=[rc], out=cst[:, 448:480], in_=cst[:, 448:480], func=AF.Silu)

        self.areset()
        slots = [(self.alloc(16 * 512, BF16), Res()) for _ in range(3)]
        gb = [[self.alloc(D, F32) for g in range(2)] for v in range(2)]
        rgb = Res()
        scB = self.alloc(16 * 2 * 128, F32).rearrange("p (k v m) -> p k v m", k=16, v=2)
        rscB = Res()
        for v in range(2):
            I("dve", "tensor_copy", reads=[rc], writes=[rscB], out=scB[:, :, v, :],
              in_=scT[:, :, v:v + 1].broadcast_to([128, 16, 128]))
        for v in range(2):
            for g in range(2):
                I("sp", "dma_start", writes=[rgb], out=gb[v][g], in_=bmg[g:g + 1, :].broadcast_to([128, D]))
        wm3 = w_mod.rearrange("(kc p) n -> p kc n", p=128)
        fm_cols = [0, 2048, 6144, 8192]
        tiles = []
        for gi in range(4):
            for t in range(8):
                tiles.append(("fm", gi * 16 + t * 2, fm_cols[gi] + t * 256))
        for g, c0 in enumerate([4096, 10240]):
            for t in range(8):
                tiles.append(("gb", g, c0 + t * 256, t))
        for ti, tl in enumerate(tiles):
            ap, r = slots[ti % 3]
            wv = ap.bitcast(F32).rearrange("p (kc n) -> p kc n", n=256)
            I("sp", "dma_start", writes=[r], out=wv, in_=wm3[:, :, tl[2]:tl[2] + 256])
            for v in range(2):
                pg, _, rpg = self.bank()
                for kc in range(16):
                    I("pe", "matmul", reads=[r, rscB], writes=[rpg], out=pg[:, 0:256],
                      lhsT=scB[:, kc, v, :], rhs=wv[:, kc, :], start=(kc == 0), stop=(kc == 15))
                if tl[0] == "fm":
                    for ch in range(2):
                        j = tl[1] + ch
                        I("dve", "scalar_tensor_tensor", reads=[rpg, rc], writes=[rc, rdj], out=djunk,
                          in0=pg[:, ch * 128:(ch + 1) * 128], scalar=1.0, in1=identF, op0=ALU.mult, op1=ALU.mult,
                          accum_out=modT[:, j, v:v + 1])
                else:
                    g, c0, t = tl[1], tl[2], tl[3]
                    I("dve", "tensor_tensor", reads=[rpg, rgb], writes=[rgb],
                      out=gb[v][g][:, t * 256:(t + 1) * 256], in0=pg[:, 0:256],
                      in1=gb[v][g][:, t * 256:(t + 1) * 256], op=ALU.add)
        for v in range(2):
            I("dve", "tensor_tensor", reads=[rc], writes=[rc], out=modT[:, :, v], in0=modT[:, :, v],
              in1=V("bm"), op=ALU.add)
        for v in range(2):
            I("dve", "scalar_tensor_tensor", reads=[rc], writes=[rc], out=A1[:, v, :], in0=modT[:, 16:32, v],
              scalar=1.0, in1=V("n1g"), op0=ALU.add, op1=ALU.mult)
            I("dve", "scalar_tensor_tensor", reads=[rc], writes=[rc], out=A2[:, v, :], in0=modT[:, 48:64, v],
              scalar=1.0, in1=V("n2g"), op0=ALU.add, op1=ALU.mult)
        if self.debug:
            I("sp", "dma_start", reads=[rc], out=dbg_mod[:, :], in_=cst[:, 256:256 + 192])
        gbd = self.dscr("gbd", [4, 128, D], F32, dbg=True)
        for v in range(2):
            for g in range(2):
                I("sp", "dma_start", reads=[rgb], out=gbd[v * 2 + g], in_=gb[v][g])
        stg = [(self.alloc(2048, F32), self.alloc(2048, BF16), Res(), Res()) for _ in range(4)]
        jobs = []
        for (W, Wb, K, NN) in [(w_in, wb_in, D, INC), (w_up_a, wb_upa, DR, D), (w_up_b, wb_upb, DR, D),
                               (w_out, wb_out, D, D), (w13, wb_13, D, 2 * DFF), (w2f, wb_2, DFF, D)]:
            for kc in range(K // 128):
                bounds = [0, 2048, 4096, 4480, 6528, INC] if W is w_in else list(range(0, NN, 2048)) + [NN]
                for n0, n1 in zip(bounds[:-1], bounds[1:]):
                    jobs.append((W, Wb, kc, n0, n1))
        jobs_now = [j for j in jobs if j[0] is w_in and j[3] < 4480]
        self.bg_jobs = []
        for (W, Wb, kc, n0, n1) in jobs:
            if W is w_in and n0 < 4480:
                continue
            for c0 in range(n0, n1, 512):
                self.bg_jobs.append((W, Wb, kc, c0, min(n1, c0 + 512)))
        self.convert_jobs(jobs_now, stg)
        self.P.barrier()

        self.areset()
        xt = self.alloc(4 * D, F32).rearrange("p (s d) -> p s d", s=4)
        rxt = RL(4)
        xn = self.alloc(4 * D, BF16).rearrange("p (s d) -> p s d", s=4)
        rxn = RL(4)
        hT2 = [self.alloc(16 * TT, BF16).rearrange("p (k t) -> p k t", k=16) for _ in range(2)]
        rhT2 = [RL(16), RL(16)]
        small = self.alloc(64, F32)
        rsm = Res()
        slots = [(self.alloc(16 * 512, BF16), Res()) for _ in range(3)]
        ostg = [(self.alloc(TT, F32), Res()) for _ in range(4)]
        pairs = [[0, 1], [2, 3], [4, 5], [6, 7], [8]]
        wt = []
        for _pr in pairs:
            for t in range(9):
                wt.append((wb_in, 0, 16, t * 512, min(512, 4480 - t * 512)))
        self.wstream_init(slots, wt)
        self.norm_ctx = dict(xn=xn, rxn=rxn, junk=None, rjunk=None, small=small, rsm=rsm, identB=identB, rc=rc)
        oi = 0
        for pr in pairs:
            for hi, it in enumerate(pr):
                v = 0 if it < 8 else 1
                tok0 = it * TT
                for s_ in range(4):
                    I("sp", "dma_start", writes=[rxt[s_]], out=xt[:, s_, :], in_=xall[tok0 + s_ * 128: tok0 + (s_ + 1) * 128, :])
                self.norm_T(xt, rxt, A1[:, v, :], lambda kc, v=v: modT[:, kc, v:v + 1], hT2[hi], rhT2[hi], EPS)
            for t in range(9):
                wv, rw = self.wget()
                nch = 4 if t < 8 else 3
                for hi, it in enumerate(pr):
                    tok0 = it * TT
                    for ch in range(nch):
                        j = t * 4 + ch
                        pb, _, rp = self.bank()
                        for kc in range(16):
                            I("pe", "matmul", reads=[rw, rhT2[hi][kc]], writes=[rp], out=pb[:, :],
                              lhsT=wv[:, kc, ch * 128:(ch + 1) * 128], rhs=hT2[hi][:, kc, :], start=(kc == 0), stop=(kc == 15))
                        og, rog = ostg[oi % 4]
                        oi += 1
                        e = self.ev_eng()
                        if e == "dve":
                            I("dve", "tensor_copy", reads=[rp], writes=[rog], out=og, in_=pb[:, :])
                        else:
                            I("act", "copy", reads=[rp], writes=[rog], out=og, in_=pb[:, :])
                        if j < 27:
                            dst = za[j * 128:(j + 1) * 128, tok0:tok0 + TT]
                        else:
                            dst = zb[(j - 27) * 128:(j - 26) * 128, tok0:tok0 + TT]
                        I("pool", "dma_start", reads=[rog], out=dst, in_=og)
        self.P.barrier()
        self.hd = dict(xall=xall, y=y, za=za, zb=zb, ya=ya, yb=yb, gbd=gbd, fing=fing, wb_in=wb_in, wb_upa=wb_upa,
                       wb_upb=wb_upb, wb_out=wb_out, wb_13=wb_13, wb_2=wb_2, modT=modT, A1=A1, A2=A2, identB=identB,
                       identF=identF, rc=rc, pool_w=pool_w, state=state, ns_out=ns_out, w2l=w2l, a2l=a2l, g2l=g2l,
                       cst=cst, cstb=cstb)
        if "P" in self.phases:
            self.phase_P()
            self.P.barrier()
        if "B" in self.phases:
            self.phase_B()
            self.P.barrier()
        if "C" in self.phases:
            self.phase_C()
        self.P.emit()
        return nc

    def wsum(self, eng, cur, rcur, bufs, bi, w, axis, L):
        I = self.I

        def sl(ap, a, b):
            return ap[:, :, a:b] if axis == 2 else ap[:, a:b, :]
        o, ro = bufs[bi]
        bi ^= 1
        I(eng, "tensor_tensor", reads=[rcur], writes=[ro], out=sl(o, 1, L), in0=sl(cur, 0, L - 1), in1=sl(cur, 1, L), op=ALU.add)
        cur, rcur = o, ro
        lo, sh, k = 1, 1, 2
        while k < w:
            o, ro = bufs[bi]
            bi ^= 1
            I(eng, "tensor_tensor", reads=[rcur], writes=[ro], out=sl(o, lo + sh, L - lo - sh),
              in0=sl(cur, lo, L - lo - 2 * sh), in1=sl(cur, lo + 2 * sh, L - lo), op=ALU.add)
            cur, rcur = o, ro
            lo += sh
            sh *= 2
            k *= 2
        return cur, rcur, bi

    def phase_P(self):
        I, hd, V = self.I, self.hd, self.V
        zb, yb, pool_w, rc = hd["zb"], hd["yb"], hd["pool_w"], hd["rc"]
        invc = self.dr["invc"]
        self.areset()
        GW = 80
        LW = 272
        raw = [[self.alloc(GW * GW, F32) for _ in range(3)] for _ in range(2)]
        rr_ = [RL(3) for _ in range(2)]
        ic = self.alloc(4096, F32)
        ric = Res()
        dT = [self.alloc(4096, BF16) for _ in range(2)]
        rdT = RL(2)
        pwf = self.alloc(512, F32).rearrange("p (k n) -> p k n", k=2)
        pwb = self.alloc(512, BF16).rearrange("p (k n) -> p k n", k=2)
        rpwf, rpwb = Res(), Res()
        ost = [(self.alloc(512, BF16), Res()) for _ in range(4)]
        ws = (2, 4, 8, 16)
        for kind in ("s", "p"):
            for k in range(2):
                I("dve" if k == 0 else "pool", "memset", writes=[rr_[k][0]], ap=raw[k][0], constant=0.0)
            if kind == "s":
                view = lambda ap: ap.rearrange("p (r c) -> p r c", c=GW)
                inner = lambda ap3: ap3[:, 8:72, 8:72]
                icv = ic.rearrange("p (r c) -> p r c", c=64)
                dview = lambda ap: ap.rearrange("p (r c) -> p r c", c=64)
                tok0, ntb = 0, 8
            else:
                view = lambda ap: ap[:, 0:2 * LW].rearrange("p (r c) -> p r c", c=LW)
                inner = lambda ap3: ap3[:, :, 8:264]
                icv = ic[:, 0:512].rearrange("p (r c) -> p r c", c=256)
                dview = lambda ap: ap[:, 0:512].rearrange("p (r c) -> p r c", c=256)
                tok0, ntb = 4096, 1
            for gi in range(4):
                w = ws[gi]
                if kind == "s":
                    I("sp", "dma_start", writes=[ric], out=ic, in_=invc[gi:gi + 1, 0:4096].broadcast_to([128, 4096]))
                else:
                    I("sp", "dma_start", writes=[ric], out=ic[:, 0:512], in_=invc[4 + gi:5 + gi, 0:512].broadcast_to([128, 512]))
                I("sp", "dma_start", writes=[rpwf], out=pwf, in_=pool_w[gi].rearrange("(k p) n -> p k n", p=128))
                I("act", "copy", reads=[rpwf], writes=[rpwb], out=pwb, in_=pwf)
                for k in range(2):
                    q = gi * 2 + k
                    eng = "dve" if k == 0 else "pool"
                    G3 = view(raw[k][0])
                    if kind == "s":
                        src = zb[q * 128:(q + 1) * 128, 0:4096].rearrange("p (r c) -> p r c", c=64)
                    else:
                        src = zb[q * 128:(q + 1) * 128, 4096:4608].rearrange("p (r c) -> p r c", c=256)
                    if kind == "s":
                        for r4 in range(4):
                            I("sp", "dma_start", writes=[rr_[k][0]], out=G3[:, 8 + 16 * r4:24 + 16 * r4, 8:72], in_=src[:, 16 * r4:16 * r4 + 16, :])
                    else:
                        I("sp", "dma_start", writes=[rr_[k][0]], out=inner(G3), in_=src)
                    bufs = [(view(raw[k][1]), rr_[k][1]), (view(raw[k][2]), rr_[k][2])]
                    cur, rcur, bi = self.wsum(eng, G3, rr_[k][0], bufs, 0, w, 2, GW if kind == "s" else LW)
                    if kind == "s":
                        cur, rcur, bi = self.wsum(eng, cur, rcur, bufs, bi, w, 1, GW)
                    o, ro = bufs[bi]
                    I(eng, "tensor_tensor", reads=[rcur, ric], writes=[ro], out=inner(o), in0=inner(cur), in1=icv, op=ALU.mult)
                    I(eng, "tensor_tensor", reads=[ro, rr_[k][0]], writes=[rdT[k]], out=dview(dT[k]), in0=inner(o),
                      in1=inner(G3), op=ALU.subtract)
                for tb_ in range(ntb):
                    for oc in range(2):
                        pb, _, rp = self.bank()
                        for kc in range(2):
                            I("pe", "matmul", reads=[rpwb, rdT[kc]], writes=[rp], out=pb[:, :],
                              lhsT=pwb[:, kc, oc * 128:(oc + 1) * 128], rhs=dT[kc][:, tb_ * 512:(tb_ + 1) * 512],
                              start=(kc == 0), stop=(kc == 1))
                        og, rog = ost[(tb_ * 2 + oc) % 4]
                        qq = gi * 2 + oc
                        I("act", "activation", reads=[rp, rc], writes=[rog], out=og, in_=pb[:, :], func=AF.Copy,
                          scale=V("psc")[:, qq:qq + 1])
                        I("pool", "dma_start", reads=[rog], out=yb[qq * 128:(qq + 1) * 128, tok0 + tb_ * 512: tok0 + (tb_ + 1) * 512],
                          in_=og)

    def phase_B(self):
        I, hd, V = self.I, self.hd, self.V
        za, ya, state, ns_out, rc, identB, identF = (hd[k] for k in ("za", "ya", "state", "ns_out", "rc", "identB", "identF"))
        yfs = self.dscr("yfs", [8, 128, NTOK // 64, 64], F32)
        ryfs = [RL(NTOK // 256) for _ in range(8)]
        self.areset()
        A = self.alloc
        NB = 256
        NCH = 4
        KU = 4
        mk = A(4 * 512, F32).rearrange("p (d x) -> p d x", d=4)
        cmask = A(512, F32)
        onesbd = A(128, F32)
        istf = A(64, F32)
        istb = A(64, BF16)
        lz = [A(DR, BF16) for _ in range(4)]
        g2b_ = A(DR, BF16)
        rk_ = Res()
        sv = A(64, F32)
        sv2 = A(16, F32)
        omu, hmu = sv[:, 0:27], sv[:, 27:54]
        oka, oka2 = sv2[:, 0:8], sv2[:, 8:16]
        Sst = A(8 * 64, F32).rearrange("p (q v) -> p q v", q=8)
        rS = RL(8)
        Sbd = (A(128, F32), Res())
        nsst = (A(64, F32), Res())

        def F(n=NB):
            return A(n, F32), Res()

        def Bf(n):
            return A(n, BF16), Res()

        def mkset():
            B = {}
            B["zin"] = [F(NB + 2) for _ in range(3)]
            for nm in ("rs", "ks", "vs", "t1", "sq", "rn", "kk", "sgw", "aa", "Ls", "Lx", "Lex", "E1", "E2", "E3", "t3", "kd", "t4"):
                B[nm] = F()
            for nm, al in (("ysum", "sq"), ("yc", "rn"), ("ysq", "Lex"), ("yfl", "sgw"), ("t6", "Ls"), ("t7", "Lx"), ("t8", "E1"), ("af", "E3")):
                B[nm] = B[al]
            B["gam"] = F(8)
            B["gst"] = F(32)
            for nm in ("KT", "BT", "VB"):
                B[nm] = Bf(NCH * 128)
            for nm in ("RA", "AK", "AB"):
                B[nm] = Bf(NCH * 256)

            def al(nm):
                return (B[nm][0].bitcast(BF16), B[nm][1])
            B["Ktm"], B["Btm"], B["N0T"] = al("sgw"), al("Ls"), al("Lex")
            B["Nb"] = [al("E1"), al("E2")]
            B["NTb"] = [al("E3"), al("t3")]
            B["Pb"] = [al("kd"), al("t4")]
            B["Vtm"] = Bf(NCH * 64)
            B["UT"] = Bf(NCH * 64)
            B["SBs"] = Bf((NCH + 1) * 64)
            B["Uraw"] = Bf(64)
            B["YN"] = (A(NCH * 128, F32), Res())
            B["ost"] = Bf(NB)
            return B

        sets = [mkset() for _ in range(KU)]
        for i_, B in enumerate(sets):
            B["banks"] = (2 * i_, 2 * i_ + 1)
            B["bk"] = 0
        shr = []
        for _ in range(2):
            shr.append(dict(zlo=[F(NB + 2) for _ in range(3)], tmp=F(), t1=F(), tw=Bf(NB), a_in=Bf(NB), sgd=Bf(NB)))
        lstage_ap, rls = sets[0]["YN"][0], sets[0]["YN"][1]
        lstage = lstage_ap
        I("sp", "dma_start", writes=[rk_], out=mk, in_=self.dr["bmask"].rearrange("d p x -> p d x"))
        I("sp", "dma_start", writes=[rk_], out=cmask, in_=self.dr["cmask"][:, :])
        I("sp", "dma_start", writes=[rk_], out=onesbd, in_=self.dr["onesbd"][:, :])
        I("sp", "dma_start", writes=[rk_], out=istf, in_=self.dr["istf"][:, :])
        I("act", "copy", reads=[rk_], writes=[rk_], out=istb, in_=istf)
        for i_, (src, d_) in enumerate([(hd["w2l"], 0), (hd["w2l"], 1), (hd["a2l"], 0), (hd["a2l"], 1)]):
            for hf in range(2):
                I("dve", "memset", writes=[rls], ap=lstage, constant=0.0)
                I("sp", "dma_start", writes=[rls], out=lstage[64 * d_:64 * d_ + 64, :], in_=src[d_][:, hf * 512:(hf + 1) * 512])
                I("dve", "tensor_copy", reads=[rls], writes=[rk_], out=lz[i_][:, hf * 512:(hf + 1) * 512], in_=lstage)
        for hf in range(2):
            I("sp", "dma_start", writes=[rls], out=lstage, in_=hd["g2l"][:, hf * 512:(hf + 1) * 512])
            I("dve", "tensor_copy", reads=[rls], writes=[rk_], out=g2b_[:, hf * 512:(hf + 1) * 512], in_=lstage)
        I("dve", "tensor_scalar", reads=[rc], writes=[rk_], out=omu, in0=V("mu"), scalar1=-1.0, scalar2=1.0, op0=ALU.mult, op1=ALU.add)
        I("dve", "tensor_scalar", reads=[rc], writes=[rk_], out=hmu, in0=V("mu"), scalar1=0.5, scalar2=None, op0=ALU.mult)
        I("dve", "tensor_scalar", reads=[rc], writes=[rk_], out=oka, in0=V("ka"), scalar1=-1.0, scalar2=1.0, op0=ALU.mult, op1=ALU.add)
        I("dve", "tensor_scalar", reads=[rc], writes=[rk_], out=oka2, in0=V("ka"), scalar1=-2.0, scalar2=2.0, op0=ALU.mult, op1=ALU.add)
        for B in sets:
            for nm in ("KT", "BT", "VB", "RA", "YN"):
                I("dve", "memset", writes=[B[nm][1]], ap=B[nm][0], constant=0.0)
        I("dve", "memset", writes=[Sbd[1]], ap=Sbd[0], constant=0.0)
        cch = self.dscr("cch", [4, 8, 128, NTOK], F32)
        rcch = [RL(NTOK // 256) for _ in range(8)]
        lch = self.dscr("lch", [3, 128, NTOK], BF16)
        rlch = RL(NTOK // 256)
        self.Bc = dict(lch=lch, rlch=rlch, cch=cch, rcch=rcch, za=za, ya=ya, state=state, ns_out=ns_out, rk_=rk_, identB=identB, identF=identF, mk=mk, cmask=cmask, onesbd=onesbd,
                       istf=istf, istb=istb, lz=lz, g2b_=g2b_, omu=omu, hmu=hmu, oka=oka, oka2=oka2, Sst=Sst, rS=rS, Sbd=Sbd, nsst=nsst,
                       yfs=yfs, ryfs=ryfs, shr=shr)
        seqs = [(0, TS), (TS, TP), (TS + TP, TP)]
        units = []
        tcount = 0
        for d_ in range(2):
            for si, (s0, slen) in enumerate(seqs):
                ntile = slen // NB
                torder = list(range(ntile)) if d_ == 0 else list(range(ntile - 1, -1, -1))
                for idx, ti in enumerate(torder):
                    for p in range(8):
                        units.append(dict(d=d_, si=si, s0=s0, s1=s0 + slen, tok0=s0 + ti * NB, p=p, first=(idx == 0),
                                          last=(idx == ntile - 1), sh=tcount % 2, lead=(p == 0)))
                    tcount += 1
        cstg = [(A(512, F32), A(512, BF16), Res(), Res()) for _ in range(3)]
        bg = self.bg_convert(self.bg_jobs, cstg)
        bg_live = True
        free = list(range(KU))
        active = []
        ui = 0
        rnd = 0
        last_start = -10 ** 9
        GAP = 0
        while ui < len(units) or active:
            if ui < len(units) and free and rnd - last_start >= GAP:
                bi = free.pop(0)
                active.append((self.rwkv_unit(units[ui], sets[bi]), bi))
                ui += 1
                last_start = rnd
            for item in list(active):
                g, bi = item
                try:
                    next(g)
                except StopIteration:
                    active.remove(item)
                    free.append(bi)
            rnd += 1
            if bg_live and rnd % 2 == 0:
                try:
                    next(bg)
                except StopIteration:
                    bg_live = False
        while bg_live:
            try:
                next(bg)
            except StopIteration:
                bg_live = False

    def bg_convert(self, jobs, stg):
        I = self.I
        n = len(jobs)
        ns = len(stg)
        for i in range(n + 2):
            k2 = i - 2
            if 0 <= k2 < n:
                W, Wb, kc, n0, n1 = jobs[k2]
                f, b, rf, rb = stg[k2 % ns]
                I("sp", "dma_start", reads=[rb], out=Wb[kc * 128:(kc + 1) * 128, n0:n1], in_=b[:, :n1 - n0])
            if i < n:
                W, Wb, kc, n0, n1 = jobs[i]
                f, b, rf, rb = stg[i % ns]
                I("sp", "dma_start", writes=[rf], out=f[:, :n1 - n0], in_=W[kc * 128:(kc + 1) * 128, n0:n1])
            k = i - 1
            if 0 <= k < n:
                W, Wb, kc, n0, n1 = jobs[k]
                f, b, rf, rb = stg[k % ns]
                if k % 2:
                    I("act", "copy", reads=[rf], writes=[rb], out=b[:, :n1 - n0], in_=f[:, :n1 - n0])
                else:
                    I("dve", "tensor_copy", reads=[rf], writes=[rb], out=b[:, :n1 - n0], in_=f[:, :n1 - n0])
            yield

    def rwkv_unit(self, u, B):
        I, V = self.I, self.V
        C = self.Bc
        za, ya, rk_, identB, identF, mk, cmask, onesbd, istf, istb, lz, g2b_, omu, hmu, oka, oka2 = (C[k] for k in (
            "za", "ya", "rk_", "identB", "identF", "mk", "cmask", "onesbd", "istf", "istb", "lz", "g2b_", "omu", "hmu", "oka", "oka2"))
        Sst, rS, Sbd, nsst, yfs, ryfs = (C[k] for k in ("Sst", "rS", "Sbd", "nsst", "yfs", "ryfs"))
        S = C["shr"][u["sh"]]
        rc = self.hd["rc"]
        d_, si, p, tok0, s0, s1 = u["d"], u["si"], u["p"], u["tok0"], u["s0"], u["s1"]
        n, nch = 256, 4
        cg0 = tok0 // 64
        zin, rs, ks, vs, t1, sq, rn, kk, sgw, aa, Ls, Lx, Lex, E1, E2, E3, t3, kd, t4, af = (B[k] for k in (
            "zin", "rs", "ks", "vs", "t1", "sq", "rn", "kk", "sgw", "aa", "Ls", "Lx", "Lex", "E1", "E2", "E3", "t3", "kd", "t4", "af"))
        gam, KT, BT, VB, RA, Ktm, Btm, AK, AB, N0T, Nb, NTb, Pb, Vtm, UT, SBs, Uraw = (B[k] for k in (
            "gam", "KT", "BT", "VB", "RA", "Ktm", "Btm", "AK", "AB", "N0T", "Nb", "NTb", "Pb", "Vtm", "UT", "SBs", "Uraw"))
        ysum, yc, ysq, yfl, YN, gst, t6, t7, t8, ost = (B[k] for k in ("ysum", "yc", "ysq", "yfl", "YN", "gst", "t6", "t7", "t8", "ost"))
        tw, a_in, sgd = S["tw"], S["a_in"], S["sgd"]

        def v3(ap, inner):
            return ap.rearrange("p (c x) -> p c x", x=inner)

        def ubank():
            i = B["banks"][B["bk"] % 2]
            B["bk"] += 1
            return self.ps[i], self.psb[i], self.rps[i]

        def shift(dst, rdst, src, rsrc, ch, tmp, eng2="pool"):
            I(eng2, "tensor_tensor", reads=[rsrc], writes=[tmp[1]], out=tmp[0][:, :n], in0=src[:, 0:n], in1=src[:, 2:n + 2], op=ALU.add)
            I("act", "activation", reads=[rsrc, rk_], writes=[rdst], out=dst[:, :n], in_=src[:, 1:n + 1], func=AF.Copy, scale=omu[:, ch:ch + 1])
            I("dve", "scalar_tensor_tensor", reads=[tmp[1], rdst, rk_], writes=[rdst], out=dst[:, :n], in0=tmp[0][:, :n],
              scalar=hmu[:, ch:ch + 1], in1=dst[:, :n], op0=ALU.mult, op1=ALU.add)

        def load_halo(buf, rbuf, row0):
            lo, hi = max(s0, tok0 - 1), min(s1, tok0 + n + 1)
            if tok0 - 1 < s0:
                I("pool", "memset", writes=[rbuf], ap=buf[:, 0:1], constant=0.0)
            if tok0 + n + 1 > s1:
                I("pool", "memset", writes=[rbuf], ap=buf[:, n + 1:n + 2], constant=0.0)
            I("sp", "dma_start", writes=[rbuf], out=buf[:, lo - (tok0 - 1): hi - (tok0 - 1)], in_=za[row0:row0 + 128, lo:hi])

        if u["lead"]:
            lch = C["lch"]
            rlc = C["rlch"][tok0 // 256]
            if d_ == 0:
                zlo, tmp, st1 = S["zlo"], S["tmp"], S["t1"]
                for j in range(3):
                    load_halo(zlo[j][0], zlo[j][1], 3072 + j * 128)
                shift(tmp[0], tmp[1], zlo[0][0], zlo[0][1], 24, st1)
                I("act", "activation", reads=[tmp[1]], writes=[tw[1]], out=tw[0][:, :n], in_=tmp[0][:, :n], func=AF.Tanh)
                shift(tmp[0], tmp[1], zlo[1][0], zlo[1][1], 25, st1)
                I("act", "copy", reads=[tmp[1]], writes=[a_in[1]], out=a_in[0][:, :n], in_=tmp[0][:, :n])
                shift(tmp[0], tmp[1], zlo[2][0], zlo[2][1], 26, st1)
                I("act", "activation", reads=[tmp[1]], writes=[sgd[1]], out=sgd[0][:, :n], in_=tmp[0][:, :n], func=AF.Sigmoid)
                for q_, src_ in enumerate((tw, a_in, sgd)):
                    I("sp", "dma_start", reads=[src_[1]], writes=[rlc], out=lch[q_, :, tok0:tok0 + n], in_=src_[0][:, :n])
            else:
                for q_, dst_ in enumerate((tw, a_in, sgd)):
                    I("sp", "dma_start", reads=[rlc], writes=[dst_[1]], out=dst_[0][:, :n], in_=lch[q_, :, tok0:tok0 + n])
        if u["first"]:
            if si == 0:
                for l in range(2):
                    I("sp", "dma_start", writes=[Sbd[1]], out=Sbd[0][64 * l:64 * l + 64, 64 * l:64 * l + 64], in_=C["state"][d_, 2 * p + l])
                pb, _, rp = ubank()
                I("pe", "matmul", reads=[Sbd[1], rk_], writes=[rp], out=pb[:, 0:64], lhsT=Sbd[0], rhs=istf, start=True, stop=True)
                I("dve", "tensor_copy", reads=[rp], writes=[rS[p]], out=Sst[:, p, :], in_=pb[:, 0:64])
            else:
                I("dve", "memset", writes=[rS[p]], ap=Sst[:, p, :], constant=0.0)
        cch = C["cch"]
        rcc = C["rcch"][p][tok0 // 256]
        if d_ == 0:
            for j, (dst, base) in enumerate([(rs, 0), (ks, 1024), (vs, 2048)]):
                load_halo(zin[j][0], zin[j][1], base + p * 128)
            yield
            for j, (dst, base) in enumerate([(rs, 0), (ks, 1024), (vs, 2048)]):
                shift(dst[0], dst[1], zin[j][0], zin[j][1], (base // 128) + p, t1)
                yield
            I("act", "activation", reads=[ks[1], rc], writes=[sq[1]], out=sq[0][:, :n], in_=ks[0][:, :n], func=AF.Square, scale=V("kk")[:, p:p + 1])
        else:
            for q_, dst in enumerate((rs, ks, vs, kk)):
                I("sp", "dma_start", reads=[rcc], writes=[dst[1]], out=dst[0][:, :n], in_=cch[q_, p, :, tok0:tok0 + n])
        pbw, _, rpw = ubank()
        I("pe", "matmul", reads=[tw[1], rk_], writes=[rpw], out=pbw[:, 0:n], lhsT=lz[d_][:, p * 128:(p + 1) * 128], rhs=tw[0][:, :n], start=True, stop=True)
        I("pe", "matmul", reads=[a_in[1], rk_], writes=[rpw], out=pbw[:, 256:256 + n], lhsT=lz[2 + d_][:, p * 128:(p + 1) * 128], rhs=a_in[0][:, :n], start=True, stop=True)
        if d_ == 0:
            pb, _, rp = ubank()
            I("pe", "matmul", reads=[sq[1], rk_], writes=[rp], out=pb[:, :n], lhsT=onesbd, rhs=sq[0][:, :n], start=True, stop=True)
        yield
        I("act", "activation", reads=[rpw, rc], writes=[sgw[1]], out=sgw[0][:, :n], in_=pbw[:, 0:n], func=AF.Sigmoid,
          bias=V("w0")[:, d_ * 8 + p:d_ * 8 + p + 1])
        I("act", "activation", reads=[rpw, rc], writes=[aa[1]], out=aa[0][:, :n], in_=pbw[:, 256:256 + n], func=AF.Sigmoid,
          bias=V("a0")[:, d_ * 8 + p:d_ * 8 + p + 1])
        yield
        if d_ == 0:
            I("act", "activation", reads=[rp], writes=[rn[1]], out=rn[0][:, :n], in_=pb[:, :n], func=AF.Sqrt)
            yield
        I("dve", "tensor_tensor_scan", reads=[sgw[1], rk_], writes=[Ls[1]], out=Ls[0][:, :n], data0=cmask[:, :n], data1=sgw[0][:, :n],
          initial=0.0, op0=ALU.mult, op1=ALU.add)
        yield
        if d_ == 0:
            I("dve", "tensor_scalar", reads=[rn[1]], writes=[rn[1]], out=rn[0][:, :n], in0=rn[0][:, :n], scalar1=1e-12, scalar2=None, op0=ALU.max)
            I("dve", "reciprocal", reads=[rn[1]], writes=[rn[1]], out=rn[0][:, :n], in_=rn[0][:, :n])
            yield
            I("dve", "scalar_tensor_tensor", reads=[ks[1], rn[1], rc], writes=[kk[1]], out=kk[0][:, :n], in0=ks[0][:, :n],
              scalar=V("kk")[:, p:p + 1], in1=rn[0][:, :n], op0=ALU.mult, op1=ALU.mult)
            yield
            for q_, src_ in enumerate((rs, ks, vs, kk)):
                I("sp", "dma_start", reads=[src_[1]], writes=[rcc], out=cch[q_, p, :, tok0:tok0 + n], in_=src_[0][:, :n])
        Ls3 = v3(Ls[0][:, :n], 64)
        if d_ == 0:
            LxT = Ls
        else:
            I("pool", "tensor_tensor", reads=[sgw[1], Ls[1]], writes=[Lx[1]], out=Lx[0][:, :n], in0=sgw[0][:, :n], in1=Ls[0][:, :n], op=ALU.subtract)
            I("dve", "tensor_tensor", reads=[Lx[1], Ls[1]], writes=[Lx[1]], out=v3(Lx[0][:, :n], 64), in0=v3(Lx[0][:, :n], 64),
              in1=Ls3[:, :, 63:64].broadcast_to([128, nch, 64]), op=ALU.add)
            LxT = Lx
            yield
        I("pool", "tensor_tensor", reads=[LxT[1], sgw[1]], writes=[Lex[1]], out=Lex[0][:, :n], in0=LxT[0][:, :n], in1=sgw[0][:, :n], op=ALU.subtract)
        I("act", "activation", reads=[LxT[1]], writes=[E1[1]], out=E1[0][:, :n], in_=LxT[0][:, :n], func=AF.Exp, scale=-LD)
        I("act", "activation", reads=[LxT[1]], writes=[E2[1]], out=E2[0][:, :n], in_=LxT[0][:, :n], func=AF.Exp, scale=LD)
        I("act", "activation", reads=[Ls[1]], writes=[gam[1]], out=gam[0][:, :nch], in_=Ls3[:, :, 63], func=AF.Exp, scale=-LD)
        I("act", "activation", reads=[aa[1], rc, rk_], writes=[t3[1]], out=t3[0][:, :n], in_=aa[0][:, :n], func=AF.Identity,
          scale=V("ka")[:, p:p + 1], bias=oka[:, p:p + 1])
        I("pool", "tensor_tensor", reads=[kk[1], aa[1]], writes=[t4[1]], out=t4[0][:, :n], in0=kk[0][:, :n], in1=aa[0][:, :n], op=ALU.mult)
        yield
        I("act", "activation", reads=[Lex[1]], writes=[E3[1]], out=E3[0][:, :n], in_=Lex[0][:, :n], func=AF.Exp, scale=-LD)
        I("pool", "tensor_tensor", reads=[t3[1], ks[1]], writes=[kd[1]], out=kd[0][:, :n], in0=t3[0][:, :n], in1=ks[0][:, :n], op=ALU.mult)
        yield
        KT3, BT3, VB3 = v3(KT[0], 128), v3(BT[0], 128), v3(VB[0], 128)
        RA4 = RA[0].rearrange("p (c t x) -> p c t x", t=2, x=128)
        for l in range(2):
            ps_, cs_ = slice(64 * l, 64 * l + 64), slice(64 * l, 64 * l + 64)
            e1, e2 = ("dve", "pool") if l == 0 else ("pool", "dve")
            I(e1, "tensor_tensor", reads=[kd[1], E2[1]], writes=[KT[1]], out=KT3[ps_, :nch, cs_], in0=v3(kd[0][ps_, :n], 64), in1=v3(E2[0][ps_, :n], 64), op=ALU.mult)
            I(e2, "tensor_tensor", reads=[t4[1], E2[1]], writes=[BT[1]], out=BT3[ps_, :nch, cs_], in0=v3(t4[0][ps_, :n], 64), in1=v3(E2[0][ps_, :n], 64), op=ALU.mult)
            yield
            I(e1, "tensor_tensor", reads=[rs[1], E1[1]], writes=[RA[1]], out=RA4[ps_, :nch, 0, cs_], in0=v3(rs[0][ps_, :n], 64), in1=v3(E1[0][ps_, :n], 64), op=ALU.mult)
            if l == 0:
                I("pool", "tensor_tensor", reads=[kk[1], E3[1]], writes=[E3[1]], out=E3[0][:, :n], in0=kk[0][:, :n], in1=E3[0][:, :n], op=ALU.mult)
            I("act", "activation", reads=[E3[1]], writes=[RA[1]], out=RA4[ps_, :nch, 1, cs_], in_=v3(E3[0][ps_, :n], 64), func=AF.Copy, scale=-1.0)
            I("act", "copy", reads=[vs[1]], writes=[VB[1]], out=VB3[ps_, :nch, cs_], in_=v3(vs[0][ps_, :n], 64))
            yield
        AK4 = AK[0].rearrange("p (c x) -> p c x", x=256)
        AB4 = AB[0].rearrange("p (c x) -> p c x", x=256)

        def a_mm(lh3, rl, c2):
            pb_, _, rp_ = ubank()
            for cc in range(2):
                c = c2 * 2 + cc
                I("pe", "matmul", reads=[rl, RA[1]], writes=[rp_], out=pb_[:, cc * 256:(cc + 1) * 256], lhsT=lh3[:, c, :],
                  rhs=RA4[:, c, :, :].rearrange("p t x -> p (t x)"), start=True, stop=True)
            return pb_, rp_

        def a_ev(pb_, rp_, dst4, rdst, c2):
            I("dve", "tensor_tensor", reads=[rp_, rk_], writes=[rdst], out=dst4[:, c2 * 2:c2 * 2 + 2, :].rearrange("p c x -> p (c x)"),
              in0=pb_[:, :], in1=mk[:, d_, :], op=ALU.mult)

        pbn, _, rpn = ubank()
        for c in range(nch):
            I("pe", "matmul", reads=[RA[1], BT[1]], writes=[rpn], out=pbn[:, c * 128:(c + 1) * 128], lhsT=RA4[:, c, 1, :], rhs=BT3[:, c, :], start=True, stop=True)
        pb0, rp0 = a_mm(BT3, BT[1], 0)
        yield
        I("dve", "tensor_tensor", reads=[rpn, rk_], writes=[N0T[1]], out=N0T[0][:, :nch * 128], in0=pbn[:, :nch * 128],
          in1=mk[:, 2 + d_, :nch * 128], op=ALU.mult)
        a_ev(pb0, rp0, AB4, AB[1], 0)
        pb1_, rp1_ = a_mm(BT3, BT[1], 1)
        _, pbk, rpk = ubank()
        for c in range(nch):
            I("pe", "transpose", reads=[KT[1], rc], writes=[rpk], out=pbk[:, c * 128:(c + 1) * 128], in_=KT3[:, c, :], identity=identB)
        yield
        a_ev(pb1_, rp1_, AB4, AB[1], 1)
        I("act", "copy", reads=[rpk], writes=[Ktm[1]], out=Ktm[0][:, :nch * 128], in_=pbk[:, :nch * 128])

        def d_btm():
            _, pbk_, rpk_ = ubank()
            for c in range(nch):
                I("pe", "transpose", reads=[BT[1], rc], writes=[rpk_], out=pbk_[:, c * 128:(c + 1) * 128], in_=BT3[:, c, :], identity=identB)
            return lambda: I("act", "copy", reads=[rpk_], writes=[Btm[1]], out=Btm[0][:, :nch * 128], in_=pbk_[:, :nch * 128])

        def d_vtm():
            pbv_, _, rpv_ = ubank()
            for c in range(nch):
                I("pe", "matmul", reads=[VB[1], rk_], writes=[rpv_], out=pbv_[:, c * 64:(c + 1) * 64], lhsT=VB3[:, c, :], rhs=istb, start=True, stop=True)
            return lambda: I("act", "copy", reads=[rpv_], writes=[Vtm[1]], out=Vtm[0][:, :nch * 64], in_=pbv_[:, :nch * 64])

        def d_ak(c2):
            pb_, rp_ = a_mm(KT3, KT[1], c2)
            return lambda: a_ev(pb_, rp_, AK4, AK[1], c2)

        deferred = [d_btm, d_vtm, lambda: d_ak(0), lambda: d_ak(1)]
        W_ = nch * 128
        P0 = Pb[0]
        I("pool", "tensor_tensor", reads=[AB[1], rc], writes=[P0[1]], out=v3(P0[0][:, :W_], 128), in0=AB4[:, :nch, 128:256],
          in1=identB.rearrange("p (o x) -> p o x", o=1).broadcast_to([128, nch, 128]), op=ALU.add)
        Ncur = (AB4[:, :nch, 128:256], AB[1])
        NTcur = (v3(N0T[0][:, :W_], 128), N0T[1])
        Pcur = P0
        pi = 0
        for j in range(1, 6):
            NTn = NTb[j % 2]
            Nn = Nb[j % 2]
            pb1, _, rp1 = ubank()
            for c in range(nch):
                I("pe", "matmul", reads=[Ncur[1], NTcur[1]], writes=[rp1], out=pb1[:, c * 128:(c + 1) * 128], lhsT=Ncur[0][:, c, :], rhs=NTcur[0][:, c, :], start=True, stop=True)
            if j < 5:
                pb2, _, rp2 = ubank()
                for c in range(nch):
                    I("pe", "matmul", reads=[Ncur[1], NTcur[1]], writes=[rp2], out=pb2[:, c * 128:(c + 1) * 128], lhsT=NTcur[0][:, c, :], rhs=Ncur[0][:, c, :], start=True, stop=True)
            yield
            I("act", "copy", reads=[rp1], writes=[NTn[1]], out=NTn[0][:, :W_], in_=pb1[:, :W_])
            if j < 5:
                I("act", "copy", reads=[rp2], writes=[Nn[1]], out=Nn[0][:, :W_], in_=pb2[:, :W_])
            NTcur = (v3(NTn[0][:, :W_], 128), NTn[1])
            if j < 5:
                Ncur = (v3(Nn[0][:, :W_], 128), Nn[1])
            Pn = Pb[(pi + 1) % 2]
            P3 = v3(Pcur[0][:, :W_], 128)
            pb3, _, rp3 = ubank()
            for c in range(nch):
                I("pe", "matmul", reads=[NTcur[1], Pcur[1]], writes=[rp3], out=pb3[:, c * 128:(c + 1) * 128], lhsT=NTcur[0][:, c, :], rhs=P3[:, c, :], start=True, stop=True)
            dev = deferred[j - 1]() if j <= 4 else None
            yield
            I("dve", "tensor_tensor", reads=[rp3, Pcur[1]], writes=[Pn[1]], out=Pn[0][:, :W_], in0=pb3[:, :W_], in1=Pcur[0][:, :W_], op=ALU.add)
            if dev is not None:
                dev()
            Pcur = Pn
            pi += 1
        Pf3 = v3(Pcur[0][:, :W_], 128)
        Vt3 = v3(Vtm[0][:, :nch * 64], 64)
        UT3 = v3(UT[0][:, :nch * 64], 64)
        SB3 = v3(SBs[0], 64)
        Kt3, Bt3 = v3(Ktm[0], 128), v3(Btm[0], 128)
        corder = list(range(nch)) if d_ == 0 else list(range(nch - 1, -1, -1))
        I("act", "copy", reads=[rS[p]], writes=[SBs[1]], out=SB3[:, corder[0], :], in_=Sst[:, p, :])
        yield
        for ci, c in enumerate(corder):
            pb, _, rp = ubank()
            I("pe", "matmul", reads=[RA[1], SBs[1]], writes=[rp], out=pb[:, 0:64], lhsT=RA4[:, c, 1, :], rhs=SB3[:, c, :], start=True, stop=False)
            I("pe", "matmul", reads=[AK[1], Vtm[1]], writes=[rp], out=pb[:, 0:64], lhsT=AK4[:, c, 128:256], rhs=Vt3[:, c, :], start=False, stop=True)
            yield
            I("act", "copy", reads=[rp], writes=[Uraw[1]], out=Uraw[0], in_=pb[:, 0:64])
            pb, _, rp = ubank()
            I("pe", "matmul", reads=[Pcur[1], Uraw[1]], writes=[rp], out=pb[:, 0:64], lhsT=Pf3[:, c, :], rhs=Uraw[0], start=True, stop=True)
            yield
            I("act", "copy", reads=[rp], writes=[UT[1]], out=UT3[:, c, :], in_=pb[:, 0:64])
            pb, _, rp = ubank()
            I("pe", "matmul", reads=[Ktm[1], Vtm[1]], writes=[rp], out=pb[:, 0:64], lhsT=Kt3[:, c, :], rhs=Vt3[:, c, :], start=True, stop=False)
            I("pe", "matmul", reads=[Btm[1], UT[1]], writes=[rp], out=pb[:, 0:64], lhsT=Bt3[:, c, :], rhs=UT3[:, c, :], start=False, stop=True)
            yield
            I("dve", "tensor_tensor", reads=[rp, rS[p]], writes=[rS[p]], out=Sst[:, p, :], in0=pb[:, 0:64], in1=Sst[:, p, :], op=ALU.add)
            if ci + 1 < nch:
                I("act", "activation", reads=[rS[p], gam[1]], writes=[SBs[1]], out=SB3[:, corder[ci + 1], :], in_=Sst[:, p, :], func=AF.Copy,
                  scale=gam[0][:, c:c + 1])
            I("act", "activation", reads=[rS[p], gam[1]], writes=[rS[p]], out=Sst[:, p, :], in_=Sst[:, p, :], func=AF.Copy, scale=gam[0][:, c:c + 1])
        if u["last"] and si > 0:
            for l in range(2):
                I("dve", "tensor_copy", reads=[rS[p]], writes=[Sbd[1]], out=Sbd[0][64 * l:64 * l + 64, 64 * l:64 * l + 64],
                  in_=Sst[64 * l:64 * l + 64, p, :])
            pb, _, rp = ubank()
            I("pe", "matmul", reads=[Sbd[1], rk_], writes=[rp], out=pb[:, 0:64], lhsT=Sbd[0], rhs=istf, start=True, stop=True)
            I("act", "copy", reads=[rp], writes=[nsst[1]], out=nsst[0], in_=pb[:, 0:64])
            I("sp", "dma_start", reads=[nsst[1]], out=C["ns_out"][si - 1, d_, 2 * p:2 * p + 2].rearrange("l v k -> (l v) k"), in_=nsst[0])
        pby, _, rpy = ubank()
        for c in range(nch):
            I("pe", "matmul", reads=[RA[1], SBs[1]], writes=[rpy], out=pby[:, c * 64:(c + 1) * 64], lhsT=RA4[:, c, 0, :], rhs=SB3[:, c, :], start=True, stop=False)
            I("pe", "matmul", reads=[AK[1], Vtm[1]], writes=[rpy], out=pby[:, c * 64:(c + 1) * 64], lhsT=AK4[:, c, 0:128], rhs=Vt3[:, c, :], start=False, stop=False)
            I("pe", "matmul", reads=[AB[1], UT[1]], writes=[rpy], out=pby[:, c * 64:(c + 1) * 64], lhsT=AB4[:, c, 0:128], rhs=UT3[:, c, :], start=False, stop=True)
        yslot = yfs[p, :, cg0:cg0 + nch, :]
        ry = ryfs[p][tok0 // 256]
        yield
        if d_ == 0:
            I("act", "copy", reads=[rpy], writes=[ysum[1]], out=ysum[0][:, :n], in_=pby[:, :n])
            yield
            I("sp", "dma_start", reads=[ysum[1]], writes=[ry], out=yslot, in_=v3(ysum[0][:, :n], 64))
            return
        I("sp", "dma_start", reads=[ry], writes=[yfl[1]], out=v3(yfl[0][:, :n], 64), in_=yslot)
        pbA, _, rpA = ubank()
        I("pe", "matmul", reads=[a_in[1], rk_], writes=[rpA], out=pbA[:, :n], lhsT=lz[2][:, p * 128:(p + 1) * 128], rhs=a_in[0][:, :n], start=True, stop=True)
        yield
        I("act", "activation", reads=[rpA, rc], writes=[af[1]], out=af[0][:, :n], in_=pbA[:, :n], func=AF.Sigmoid, bias=V("a0")[:, p:p + 1])
        I("dve", "tensor_tensor", reads=[rpy, yfl[1]], writes=[ysum[1]], out=ysum[0][:, :n], in0=pby[:, :n], in1=yfl[0][:, :n], op=ALU.add)
        yield
        ys3 = v3(ysum[0][:, :n], 64)
        yc3 = v3(yc[0][:, :n], 64)
        G_ = gst[0]
        I("dve", "tensor_reduce", reads=[ysum[1]], writes=[gst[1]], out=G_[:, 0:nch], in_=ys3, axis=AX.X, op=ALU.add)
        I("pool", "tensor_tensor", reads=[af[1], aa[1]], writes=[af[1]], out=af[0][:, :n], in0=af[0][:, :n], in1=aa[0][:, :n], op=ALU.add)
        yield
        I("dve", "tensor_scalar", reads=[gst[1]], writes=[gst[1]], out=G_[:, 8:8 + nch], in0=G_[:, 0:nch], scalar1=1.0 / 64, scalar2=None, op0=ALU.mult)
        I("dve", "tensor_scalar", reads=[af[1], rc, rk_], writes=[af[1]], out=af[0][:, :n], in0=af[0][:, :n], scalar1=V("ka")[:, p:p + 1],
          scalar2=oka2[:, p:p + 1], op0=ALU.mult, op1=ALU.add)
        yield
        I("dve", "tensor_tensor", reads=[ysum[1], gst[1]], writes=[yc[1]], out=yc3, in0=ys3,
          in1=G_[:, 8:8 + nch].rearrange("p (c o) -> p c o", o=1).broadcast_to([128, nch, 64]), op=ALU.subtract)
        I("pool", "tensor_tensor", reads=[af[1], ks[1]], writes=[af[1]], out=af[0][:, :n], in0=af[0][:, :n], in1=ks[0][:, :n], op=ALU.mult)
        yield
        I("pool", "tensor_tensor", reads=[yc[1]], writes=[ysq[1]], out=ysq[0][:, :n], in0=yc[0][:, :n], in1=yc[0][:, :n], op=ALU.mult)
        I("dve", "scalar_tensor_tensor", reads=[rs[1], af[1], rc], writes=[t7[1]], out=t7[0][:, :n], in0=rs[0][:, :n], scalar=V("rk")[:, p:p + 1],
          in1=af[0][:, :n], op0=ALU.mult, op1=ALU.mult)
        yield
        I("dve", "tensor_reduce", reads=[ysq[1]], writes=[gst[1]], out=G_[:, 16:16 + nch], in_=v3(ysq[0][:, :n], 64), axis=AX.X, op=ALU.add)
        pbB, _, rpB = ubank()
        I("pe", "matmul", reads=[t7[1], rk_], writes=[rpB], out=pbB[:, :n], lhsT=onesbd, rhs=t7[0][:, :n], start=True, stop=True)
        yield
        I("act", "activation", reads=[gst[1]], writes=[gst[1]], out=G_[:, 24:24 + nch], in_=G_[:, 16:16 + nch], func=AF.Sqrt, scale=1.0 / 64, bias=GN_EPS)
        I("dve", "tensor_tensor", reads=[rpB, vs[1]], writes=[t8[1]], out=t8[0][:, :n], in0=pbB[:, :n], in1=vs[0][:, :n], op=ALU.mult)
        yield
        I("dve", "reciprocal", reads=[gst[1]], writes=[gst[1]], out=G_[:, 24:24 + nch], in_=G_[:, 24:24 + nch])
        yield
        YN3 = v3(YN[0], 128)
        for l in range(2):
            ps_ = slice(64 * l, 64 * l + 64)
            I("dve" if l == 0 else "pool", "tensor_tensor", reads=[yc[1], gst[1]], writes=[YN[1]], out=YN3[ps_, :nch, ps_], in0=yc3[ps_],
              in1=G_[ps_, 24:24 + nch].rearrange("p (c o) -> p c o", o=1).broadcast_to([64, nch, 64]), op=ALU.mult)
        yield
        pbf, _, rpf = ubank()
        for c in range(nch):
            I("pe", "matmul", reads=[YN[1], rk_], writes=[rpf], out=pbf[:, c * 64:(c + 1) * 64], lhsT=YN3[:, c, :], rhs=istf, start=True, stop=True)
        yield
        I("act", "activation", reads=[rpf, rc], writes=[t6[1]], out=t6[0][:, :n], in_=pbf[:, :n], func=AF.Identity,
          scale=V("lng")[:, p:p + 1], bias=V("lnb")[:, p:p + 1])
        yield
        I("pool", "tensor_tensor", reads=[t8[1], t6[1]], writes=[t8[1]], out=t8[0][:, :n], in0=t8[0][:, :n], in1=t6[0][:, :n], op=ALU.add)
        pbG, _, rpG = ubank()
        I("pe", "matmul", reads=[sgd[1], rk_], writes=[rpG], out=pbG[:, :n], lhsT=g2b_[:, p * 128:(p + 1) * 128], rhs=sgd[0][:, :n], start=True, stop=True)
        yield
        og, rog = ost
        I("dve", "tensor_tensor", reads=[rpG, t8[1]], writes=[rog], out=og[:, :n], in0=pbG[:, :n], in1=t8[0][:, :n], op=ALU.mult)
        yield
        I("sp", "dma_start", reads=[rog], out=ya[p * 128:(p + 1) * 128, tok0:tok0 + n], in_=og[:, :n])

    def phase_C(self):
        I, hd, V = self.I, self.hd, self.V
        xall, y, ya, yb, gbd, fing = (hd[k] for k in ("xall", "y", "ya", "yb", "gbd", "fing"))
        modT, A1, A2, identB, rc = (hd[k] for k in ("modT", "A1", "A2", "identB", "rc"))
        self.areset()
        xt = self.alloc(4 * D, F32).rearrange("p (s d) -> p s d", s=4)
        rxt = RL(4)
        xnraw = self.alloc(4 * D, BF16)
        xn = xnraw.rearrange("p (s d) -> p s d", s=4)
        rxn = RL(4)
        stg = xnraw.bitcast(F32).rearrange("p (k t) -> p k t", k=8)
        hT = self.alloc(16 * TT, BF16).rearrange("p (k t) -> p k t", k=16)
        rhT = RL(16)
        small = self.alloc(64, F32)
        rsm = Res()
        slots = [(self.alloc(16 * 512, BF16), Res()) for _ in range(3)]
        U = self.alloc(48 * TT, BF16).rearrange("p (k t) -> p k t", k=48)
        rU = RL(48)
        g1b = self.alloc(D, F32)
        g2b = self.alloc(D, F32)
        fgb = self.alloc(D, F32)
        rg = Res()
        I("sp", "dma_start", writes=[rg], out=fgb, in_=fing[0:1, :].broadcast_to([128, D]))
        wt = []
        for it in range(NT):
            for g in range(4):
                wt.append((hd["wb_in"], 0, 16, 4480 + g * 512, 512))
                wt.append((hd["wb_upa"], 0, 8, g * 512, 512))
                wt.append((hd["wb_in"], 0, 16, 4480 + 2048 + g * 512, 512))
                wt.append((hd["wb_upb"], 0, 8, g * 512, 512))
            for n in range(4):
                wt.append((hd["wb_out"], 0, 16, n * 512, 512))
            for g in range(11):
                wt.append((hd["wb_13"], 0, 16, g * 512, 512))
                wt.append((hd["wb_13"], 0, 16, DFF + g * 512, 512))
            for n in range(4):
                for kg in range(3):
                    wt.append((hd["wb_2"], kg * 16, 16 if kg < 2 else 12, n * 512, 512))
        self.wstream_init(slots, wt)
        self.norm_ctx = dict(xn=xn, rxn=rxn, junk=None, rjunk=None, small=small, rsm=rsm, identB=identB, rc=rc)
        for it in range(NT):
            v = 0 if it < 8 else 1
            tok0 = it * TT
            if it in (0, 8):
                I("sp", "dma_start", writes=[rg], out=g1b, in_=gbd[v * 2 + 0])
                I("sp", "dma_start", writes=[rg], out=g2b, in_=gbd[v * 2 + 1])
            for s in range(4):
                I("sp", "dma_start", writes=[rxt[s]], out=xt[:, s, :], in_=xall[tok0 + s * 128: tok0 + (s + 1) * 128, :])
            self.norm_T(xt, rxt, A1[:, v, :], lambda kc, v=v: modT[:, kc, v:v + 1], hT, rhT, EPS)
            yaT, ybT = U[:, 16:24, :], U[:, 24:32, :]
            I("sp", "dma_start", writes=rU[16:24], out=yaT, in_=ya.rearrange("(k p) t -> p k t", p=128)[:, :, tok0:tok0 + TT])
            I("sp", "dma_start", writes=rU[24:32], out=ybT, in_=yb.rearrange("(k p) t -> p k t", p=128)[:, :, tok0:tok0 + TT])
            for g in range(4):
                for half, (ysl, nk) in enumerate([(16, 8), (24, 8)]):
                    wv, rw = self.wget()
                    for jj in range(4):
                        pb, _, rp = self.bank()
                        for kc in range(16):
                            I("pe", "matmul", reads=[rw, rhT[kc]], writes=[rp], out=pb[:, :],
                              lhsT=wv[:, kc, jj * 128:(jj + 1) * 128], rhs=hT[:, kc, :], start=(kc == 0), stop=(kc == 15))
                        k = half * 4 + jj
                        I("act", "activation", reads=[rp], writes=[rxn[k // 2]], out=stg[:, k, :], in_=pb[:, :], func=AF.Sigmoid)
                    wv, rw = self.wget()
                    for jj in range(4):
                        pb, _, rp = self.bank()
                        for kc in range(8):
                            I("pe", "matmul", reads=[rw, rU[ysl + kc]], writes=[rp], out=pb[:, :],
                              lhsT=wv[:, kc, jj * 128:(jj + 1) * 128], rhs=U[:, ysl + kc, :], start=(kc == 0), stop=(kc == 7))
                        k = half * 4 + jj
                        I("dve", "tensor_tensor", reads=[rp, rxn[k // 2]], writes=[rxn[k // 2]], out=stg[:, k, :], in0=pb[:, :],
                          in1=stg[:, k, :], op=ALU.mult)
                for jj in range(4):
                    j = g * 4 + jj
                    I("pool", "tensor_tensor", reads=[rxn[jj // 2], rxn[2 + jj // 2]], writes=[rU[j]], out=U[:, j, :],
                      in0=stg[:, jj, :], in1=stg[:, 4 + jj, :], op=ALU.add)
            self.tm_out(U, rU, [(16, 0)], 4, g1b, rg, xt, rxt, stg, rxn)
            self.norm_T(xt, rxt, A2[:, v, :], lambda kc, v=v: modT[:, 32 + kc, v:v + 1], hT, rhT, EPS)
            for g in range(11):
                wv, rw = self.wget()
                for jj in range(4):
                    pb, _, rp = self.bank()
                    for kc in range(16):
                        I("pe", "matmul", reads=[rw, rhT[kc]], writes=[rp], out=pb[:, :],
                          lhsT=wv[:, kc, jj * 128:(jj + 1) * 128], rhs=hT[:, kc, :], start=(kc == 0), stop=(kc == 15))
                    I("act", "activation", reads=[rp], writes=[rxn[jj // 2]], out=stg[:, jj, :], in_=pb[:, :], func=AF.Silu)
                wv, rw = self.wget()
                for jj in range(4):
                    pb, _, rp = self.bank()
                    for kc in range(16):
                        I("pe", "matmul", reads=[rw, rhT[kc]], writes=[rp], out=pb[:, :],
                          lhsT=wv[:, kc, jj * 128:(jj + 1) * 128], rhs=hT[:, kc, :], start=(kc == 0), stop=(kc == 15))
                    j = g * 4 + jj
                    I("dve", "tensor_tensor", reads=[rp, rxn[jj // 2]], writes=[rU[j]], out=U[:, j, :], in0=pb[:, :],
                      in1=stg[:, jj, :], op=ALU.mult)
            self.tm_out(U, rU, [(16, 0), (16, 16), (12, 32)], 4, g2b, rg, xt, rxt, stg, rxn)
            for s in range(4):
                I("act", "activation", reads=[rxt[s]], writes=[rxn[s], rsm], out=xn[:, s, :], in_=xt[:, s, :], func=AF.Square,
                  accum_out=small[:, 16 + s:17 + s])
            I("act", "activation", reads=[rsm], writes=[rsm], out=small[:, 20:24], in_=small[:, 16:20], func=AF.Sqrt,
              scale=1.0 / D, bias=EPS)
            I("dve", "reciprocal", reads=[rsm], writes=[rsm], out=small[:, 24:28], in_=small[:, 20:24])
            for s in range(4):
                I("dve", "scalar_tensor_tensor", reads=[rxt[s], rsm, rg], writes=[rxt[s]], out=xt[:, s, :], in0=xt[:, s, :],
                  scalar=small[:, 24 + s:25 + s], in1=fgb, op0=ALU.mult, op1=ALU.mult)
                I("pool", "dma_start", reads=[rxt[s]], out=y[tok0 + s * 128: tok0 + (s + 1) * 128, :], in_=xt[:, s, :])

    def tm_out(self, U, rU, kgroups, ncol, gbt, rg, xt, rxt, stg, rxn):
        I = self.I
        for n in range(ncol):
            banks = [self.bank() for _ in range(4)]
            ngr = len(kgroups)
            for gi_, (kcn, k0) in enumerate(kgroups):
                wv, rw = self.wget()
                for s in range(4):
                    pb, _, rp = banks[s]
                    for kc in range(kcn):
                        I("pe", "matmul", reads=[rw, rU[k0 + kc]], writes=[rp], out=pb[:, :],
                          lhsT=U[:, k0 + kc, s * 128:(s + 1) * 128], rhs=wv[:, kc, :],
                          start=(gi_ == 0 and kc == 0), stop=(gi_ == ngr - 1 and kc == kcn - 1))
            for s in range(4):
                pb, _, rp = banks[s]
                k = (n * 4 + s) % 8
                I("dve", "tensor_tensor", reads=[rp, rg, rxn[k // 2]], writes=[rxn[k // 2]], out=stg[:, k, :], in0=pb[:, :],
                  in1=gbt[:, n * 512:(n + 1) * 512], op=ALU.mult)
                I("pool", "tensor_tensor", reads=[rxn[k // 2], rxt[s]], writes=[rxt[s]], out=xt[:, s, n * 512:(n + 1) * 512],
                  in0=stg[:, k, :], in1=xt[:, s, n * 512:(n + 1) * 512], op=ALU.add)

    def convert_jobs(self, jobs, stg):
        I = self.I
        n = len(jobs)
        ns = len(stg)
        engs = ["dve", "act", "pool"]
        for i in range(n + 2):
            if i < n:
                W, Wb, kc, n0, n1 = jobs[i]
                f, b, rf, rb = stg[i % ns]
                I("sp", "dma_start", writes=[rf], out=f[:, :n1 - n0], in_=W[kc * 128:(kc + 1) * 128, n0:n1])
            k = i - 2
            if k >= 0:
                W, Wb, kc, n0, n1 = jobs[k]
                f, b, rf, rb = stg[k % ns]
                e = engs[k % 3]
                if e == "act":
                    I("act", "copy", reads=[rf], writes=[rb], out=b[:, :n1 - n0], in_=f[:, :n1 - n0])
                else:
                    I(e, "tensor_copy", reads=[rf], writes=[rb], out=b[:, :n1 - n0], in_=f[:, :n1 - n0])
                I("sp", "dma_start", reads=[rb], out=Wb[kc * 128:(kc + 1) * 128, n0:n1], in_=b[:, :n1 - n0])

    def norm_T(self, xt, rxt, A, Bf, hT, rhT, eps):
        I = self.I
        c = self.norm_ctx
        xn, rxn, junk, rjunk, small, rsm, identB, rc = (c[k] for k in ("xn", "rxn", "junk", "rjunk", "small", "rsm", "identB", "rc"))
        for s in range(4):
            I("act", "activation", reads=[rxt[s]], writes=[rxn[s], rsm], out=xn[:, s, :], in_=xt[:, s, :], func=AF.Square,
              accum_out=small[:, s:s + 1])
        I("act", "activation", reads=[rsm], writes=[rsm], out=small[:, 4:8], in_=small[:, 0:4], func=AF.Sqrt,
          scale=1.0 / D, bias=eps)
        I("dve", "reciprocal", reads=[rsm], writes=[rsm], out=small[:, 8:12], in_=small[:, 4:8])
        for s in range(4):
            if s % 2 == 0:
                I("dve", "tensor_scalar", reads=[rxt[s], rsm], writes=[rxn[s]], out=xn[:, s, :], in0=xt[:, s, :],
                  scalar1=small[:, 8 + s:9 + s], scalar2=None, op0=ALU.mult)
            else:
                I("act", "activation", reads=[rxt[s], rsm], writes=[rxn[s]], out=xn[:, s, :], in_=xt[:, s, :],
                  func=AF.Copy, scale=small[:, 8 + s:9 + s])
        for kc in range(16):
            _, pbb, rp = self.bank()
            for s in range(4):
                I("pe", "transpose", reads=[rxn[s]], writes=[rp], out=pbb[:, s * 128:(s + 1) * 128],
                  in_=xn[:, s, kc * 128:(kc + 1) * 128], identity=identB)
            if kc % 2 == 0:
                I("dve", "tensor_scalar", reads=[rp, rc], writes=[rhT[kc]], out=hT[:, kc, :], in0=pbb[:, 0:512],
                  scalar1=A[:, kc:kc + 1], scalar2=Bf(kc), op0=ALU.mult, op1=ALU.add)
            else:
                I("act", "activation", reads=[rp, rc], writes=[rhT[kc]], out=hT[:, kc, :], in_=pbb[:, 0:512],
                  func=AF.Identity, scale=A[:, kc:kc + 1], bias=Bf(kc))


def _chunkT(v, nchunk):
    return np.ascontiguousarray(np.asarray(v, np.float32).reshape(nchunk, 128).T)


def _invc():
    out = np.zeros((8, 4096), np.float32)
    for gi, w in enumerate((2, 4, 8, 16)):
        def cnt(n):
            t = np.arange(n)
            return (np.clip(t - w // 2 + w, 0, n) - np.clip(t - w // 2, 0, n)).astype(np.float32)
        c64 = cnt(64)
        out[gi] = (1.0 / (c64[:, None] * c64[None, :])).reshape(-1)
        c256 = 1.0 / cnt(256)
        out[4 + gi, 0:512] = np.concatenate([c256, c256])
    return out


def make_inputs(core, inp):
    i = core
    m = {}
    m["xall"] = np.ascontiguousarray(np.concatenate(
        [inp["x_sample"][i], inp["x_prompt"][2 * i], inp["x_prompt"][2 * i + 1]], axis=0), dtype=np.float32)
    cc = np.stack([inp["c"][i], inp["c_ctx"]], axis=-1)
    m["cT"] = np.ascontiguousarray(cc.reshape(16, 128, 2).transpose(1, 0, 2).reshape(128, 32), dtype=np.float32)
    m["state"] = np.ascontiguousarray(inp["state_rwkv"][i, 0], dtype=np.float32)
    bm = inp["b_mod"][0]
    vec = {
        "n1g": _chunkT(inp["norm1_g"][0], 16), "n2g": _chunkT(inp["norm2_g"][0], 16),
        "bm": np.concatenate([_chunkT(bm[0:2048], 16), _chunkT(bm[2048:4096], 16),
                              _chunkT(bm[6144:8192], 16), _chunkT(bm[8192:10240], 16)], axis=1),
        "mu": _chunkT(inp["shift_mu"][0], 27),
        "w0": np.concatenate([_chunkT(inp["w0"][0, 0], 8), _chunkT(inp["w0"][0, 1], 8)], axis=1),
        "a0": np.concatenate([_chunkT(inp["a0"][0, 0], 8), _chunkT(inp["a0"][0, 1], 8)], axis=1),
        "kk": _chunkT(inp["k_k"][0], 8), "ka": _chunkT(inp["k_a"][0], 8),
        "rk": _chunkT(inp["r_k"][0].reshape(-1), 8), "lng": _chunkT(inp["lnx_g"][0], 8),
        "lnb": _chunkT(inp["lnx_b"][0], 8), "psc": _chunkT(inp["pool_scale"][0], 8),
    }
    m["vecs"] = np.ascontiguousarray(np.concatenate([vec[n] for n, _ in VEC_LAYOUT], axis=1), dtype=np.float32)
    m["bmg"] = np.ascontiguousarray(np.stack([bm[4096:6144], bm[10240:12288]]), dtype=np.float32)
    m["fing"] = np.ascontiguousarray(inp["final_g"][None, :], dtype=np.float32)
    m["w_mod"] = inp["w_mod"][0]
    m["w_in"] = inp["w_in"][0]
    m["w2l"] = inp["w2"][0]
    m["a2l"] = inp["a2"][0]
    m["g2l"] = inp["g2"][0]
    m["w_up_a"] = inp["w_up_a"][0]
    m["pool_w"] = inp["pool_w"][0]
    m["w_up_b"] = inp["w_up_b"][0]
    m["w_out"] = inp["w_out"][0]
    m["w13"] = inp["ffn_w13"][0]
    m["w2f"] = inp["ffn_w2"][0]
    m["identb"] = np.eye(128, dtype=np.float32).astype(ml_dtypes.bfloat16)
    m["identf"] = np.eye(128, dtype=np.float32)
    m["invc"] = _invc()
    s_ = np.arange(128)[:, None] % 64
    t_ = np.arange(128)[None, :] % 64
    blk = (np.arange(128)[:, None] // 64) == (np.arange(128)[None, :] // 64)
    up_i = ((s_ <= t_) & blk).astype(np.float32)
    up_s = ((s_ < t_) & blk).astype(np.float32)
    lo_i = ((s_ >= t_) & blk).astype(np.float32)
    lo_s = ((s_ > t_) & blk).astype(np.float32)
    m["bmask"] = np.ascontiguousarray(np.stack([
        np.concatenate([up_i, up_s, up_i, up_s], axis=1), np.concatenate([lo_i, lo_s, lo_i, lo_s], axis=1),
        np.concatenate([lo_s] * 4, axis=1), np.concatenate([up_s] * 4, axis=1)]), dtype=np.float32)
    cm = np.ones((128, 512), np.float32)
    cm[:, 0::64] = 0.0
    m["cmask"] = cm
    m["onesbd"] = blk.astype(np.float32)
    m["istf"] = np.ascontiguousarray(np.concatenate([np.eye(64), np.eye(64)], axis=0), dtype=np.float32)
    return m


def kernel(**inputs):
    inp = {k: np.asarray(v) for k, v in inputs.items()}
    debug = bool(os.environ.get("MK_DEBUG"))
    b = Builder(debug=debug, phases=os.environ.get("MK_PHASES", "PBC"))
    nc = b.build()
    in_maps = [make_inputs(i, inp) for i in range(8)]
    res = run_bass_kernel_spmd(nc, in_maps, core_ids=list(range(8)))
    if debug:
        return res
    ys = np.stack([r["y"][:TS] for r in res.results])
    yp = np.concatenate([r["y"][TS:].reshape(2, TP, D) for r in res.results], axis=0)
    nst = np.concatenate([r["ns"] for r in res.results], axis=0)[:, None]
    return (yp.astype(np.float32), ys.astype(np.float32), nst.astype(np.float32))
```
